# Optimizing a Trainium2 kernel written in Bass

```python
import jax, jax.numpy as jnp
from jax import lax
import numpy as np

D_MODEL = 1024
BATCH = 8
SEQ = 8192
DEPTH = 1

D_MIX = D_MODEL
GDN_HEADS = 4
GDN_HEAD_DIM = 128
GDN_WIDTH = GDN_HEADS * GDN_HEAD_DIM
POOL_WINDOWS = (2, 4, 8, 16)
POOL_GROUPS = len(POOL_WINDOWS)
POOL_WIDTH = D_MIX - GDN_WIDTH
POOL_GROUP_DIM = POOL_WIDTH // POOL_GROUPS
CONV_K = 4
CHUNK = 64
D_FF = ((8 * D_MODEL // 3 + 255) // 256) * 256
D_IN = 4 * GDN_WIDTH + 2 * GDN_HEADS + POOL_WIDTH
N_MOD = 9
EPS = 1e-6

kernel_name = "hybrid_gdn_pool_macaron_adaln"


def rms_norm(x, gain):
    xf = x.astype(jnp.float32)
    y = xf * lax.rsqrt(jnp.mean(xf * xf, axis=-1, keepdims=True) + EPS)
    return (y * gain.astype(jnp.float32)).astype(x.dtype)


def l2_normalize(x):
    xf = x.astype(jnp.float32)
    return xf * lax.rsqrt(jnp.sum(xf * xf, axis=-1, keepdims=True) + EPS)


def modulate(h, shift, scale):
    return h * (1 + scale[:, None, :]) + shift[:, None, :]


def swiglu(h, w_gate, w_up, w_down):
    return (jax.nn.silu(h @ w_gate) * (h @ w_up)) @ w_down


def causal_depthwise_conv_silu(x, w):
    C = x.shape[-1]
    y = lax.conv_general_dilated(
        x, w[:, None, :].astype(x.dtype), window_strides=(1,), padding=[(CONV_K - 1, 0)],
        dimension_numbers=("NWC", "WIO", "NWC"), feature_group_count=C)
    return jax.nn.silu(y)


def gated_delta_rule_chunked(q, k, v, g, beta):
    B, T, H, Dk = q.shape
    Dv = v.shape[-1]
    N = T // CHUNK

    def chunks(t):
        t = t.reshape((B, N, CHUNK, H) + t.shape[3:])
        return jnp.moveaxis(t, 3, 1)

    q, k, v, g, beta = chunks(q), chunks(k), chunks(v), chunks(g), chunks(beta)
    g_cum = jnp.cumsum(g, axis=-1)
    causal = jnp.tril(jnp.ones((CHUNK, CHUNK), dtype=bool))
    strict = jnp.tril(jnp.ones((CHUNK, CHUNK), dtype=bool), -1)
    diff = g_cum[..., :, None] - g_cum[..., None, :]
    decay = jnp.where(causal, jnp.exp(jnp.where(causal, diff, 0.0)), 0.0)
    k_beta = k * beta[..., None]
    m = jnp.where(strict, jnp.einsum("bhnid,bhnjd->bhnij", k_beta, k) * decay, 0.0)
    a = m + jnp.eye(CHUNK, dtype=m.dtype)
    u = lax.linalg.triangular_solve(a, v * beta[..., None], left_side=True, lower=True)
    w = lax.linalg.triangular_solve(a, k_beta * jnp.exp(g_cum)[..., None], left_side=True, lower=True)
    intra = jnp.einsum("bhnid,bhnjd->bhnij", q, k) * decay
    g_last = g_cum[..., -1:]
    q_dec = q * jnp.exp(g_cum)[..., None]
    k_dec = k * jnp.exp(g_last - g_cum)[..., None]
    chunk_decay = jnp.exp(g_last[..., 0])
    xs = tuple(jnp.moveaxis(t, 2, 0) for t in (q_dec, k_dec, u, w, intra, chunk_decay))

    def step(S, inp):
        qd, kd, un, wn, an, cd = inp
        v_new = un - jnp.einsum("bhck,bhkv->bhcv", wn, S)
        o = jnp.einsum("bhck,bhkv->bhcv", qd, S) + jnp.einsum("bhij,bhjv->bhiv", an, v_new)
        S = S * cd[..., None, None] + jnp.einsum("bhck,bhcv->bhkv", kd, v_new)
        return S, o

    S0 = jnp.zeros((B, H, Dk, Dv), jnp.float32)
    _, o = lax.scan(step, S0, xs)
    return jnp.transpose(o, (1, 0, 3, 2, 4)).reshape(B, T, H, Dv)


def multiscale_causal_pool(p):
    B, T, _ = p.shape
    pf = p.astype(jnp.float32).reshape(B, T, POOL_GROUPS, POOL_GROUP_DIM)
    cs0 = jnp.concatenate([jnp.zeros_like(pf[:, :1]), jnp.cumsum(pf, axis=1)], axis=1)
    t1 = jnp.arange(1, T + 1, dtype=jnp.float32)
    outs = []
    for gi, win in enumerate(POOL_WINDOWS):
        cur = cs0[:, 1:, gi]
        lag = jnp.concatenate(
            [jnp.zeros((B, win - 1, POOL_GROUP_DIM), jnp.float32), cs0[:, :T - win + 1, gi]], axis=1)
        cnt = jnp.minimum(t1, win)[None, :, None]
        outs.append((cur - lag) / cnt - pf[:, :, gi])
    return jnp.stack(outs, axis=2)


def setup_inputs(seed: int = 0) -> dict:
    key = jax.random.key(seed)
    ks = jax.random.split(key, 24)
    f32 = jnp.float32
    L, D = DEPTH, D_MODEL

    def nrm(k, shape, scale):
        return jax.random.normal(k, shape, f32) * scale

    def gain(k, shape):
        return 1.0 + 0.02 * jax.random.normal(k, shape, f32)

    dt = jnp.exp(jax.random.uniform(ks[11], (L, GDN_HEADS), f32, np.log(1e-3), np.log(1e-1)))
    return {
        "x": nrm(ks[0], (BATCH, SEQ, D), 1.0),
        "c": nrm(ks[1], (BATCH, D), 1.0),
        "w_ada": nrm(ks[2], (L, D, N_MOD * D), 0.5 * D ** -0.5),
        "b_ada": nrm(ks[3], (L, N_MOD * D), 0.01),
        "norm_ffn1": gain(ks[4], (L, D)),
        "ffn1_gate": nrm(ks[5], (L, D, D_FF), D ** -0.5),
        "ffn1_up": nrm(ks[6], (L, D, D_FF), D ** -0.5),
        "ffn1_down": nrm(ks[7], (L, D_FF, D), D_FF ** -0.5),
        "norm_mix": gain(ks[8], (L, D)),
        "w_in": nrm(ks[9], (L, D, D_IN), D ** -0.5),
        "conv_w": nrm(ks[10], (L, CONV_K, 3 * GDN_WIDTH), CONV_K ** -0.5),
        "a_log": jnp.log(jax.random.uniform(ks[12], (L, GDN_HEADS), f32, 1.0, 16.0)),
        "dt_bias": dt + jnp.log(-jnp.expm1(-dt)),
        "gdn_norm": gain(ks[13], (L, GDN_HEAD_DIM)),
        "pool_w": nrm(ks[14], (L, POOL_GROUPS, POOL_GROUP_DIM, POOL_GROUP_DIM), POOL_GROUP_DIM ** -0.5),
        "pool_scale": gain(ks[15], (L, POOL_WIDTH)),
        "w_out": nrm(ks[16], (L, D_MIX, D), D_MIX ** -0.5),
        "norm_ffn2": gain(ks[17], (L, D)),
        "ffn2_gate": nrm(ks[18], (L, D, D_FF), D ** -0.5),
        "ffn2_up": nrm(ks[19], (L, D, D_FF), D ** -0.5),
        "ffn2_down": nrm(ks[20], (L, D_FF, D), D_FF ** -0.5),
        "final_norm": gain(ks[21], (D,)),
    }


def reference(x, c, w_ada, b_ada, norm_ffn1, ffn1_gate, ffn1_up, ffn1_down, norm_mix, w_in, conv_w,
              a_log, dt_bias, gdn_norm, pool_w, pool_scale, w_out, norm_ffn2, ffn2_gate, ffn2_up,
              ffn2_down, final_norm):
    B, T, _ = x.shape
    H, Dh, GW = GDN_HEADS, GDN_HEAD_DIM, GDN_WIDTH
    split_at = [3 * GW, 4 * GW, 4 * GW + H, 4 * GW + 2 * H]
    for l in range(DEPTH):
        mod = (jax.nn.silu(c) @ w_ada[l] + b_ada[l]).reshape(B, N_MOD, D_MODEL)

        h = modulate(rms_norm(x, norm_ffn1[l]), mod[:, 0], mod[:, 1])
        x = x + 0.5 * mod[:, 2][:, None, :] * swiglu(h, ffn1_gate[l], ffn1_up[l], ffn1_down[l])

        h = modulate(rms_norm(x, norm_mix[l]), mod[:, 3], mod[:, 4])
        proj = h @ w_in[l]
        qkv, z, b_raw, a_raw, p = jnp.split(proj, split_at, axis=-1)
        qkv = causal_depthwise_conv_silu(qkv, conv_w[l])
        q, k, v = jnp.split(qkv, 3, axis=-1)
        q = l2_normalize(q.reshape(B, T, H, Dh)) * (Dh ** -0.5)
        k = l2_normalize(k.reshape(B, T, H, Dh))
        v = v.reshape(B, T, H, Dh).astype(jnp.float32)
        beta = jax.nn.sigmoid(b_raw.astype(jnp.float32))
        g = -jnp.exp(a_log[l].astype(jnp.float32)) * jax.nn.softplus(
            a_raw.astype(jnp.float32) + dt_bias[l].astype(jnp.float32))
        o = gated_delta_rule_chunked(q, k, v, g, beta)
        o = rms_norm(o, gdn_norm[l]) * jax.nn.silu(z.reshape(B, T, H, Dh).astype(jnp.float32))
        gdn_out = o.reshape(B, T, GW).astype(x.dtype)

        pooled = multiscale_causal_pool(p)
        pooled = jnp.einsum("btgi,gio->btgo", pooled, pool_w[l].astype(jnp.float32))
        pool_out = (pooled.reshape(B, T, POOL_WIDTH) * pool_scale[l].astype(jnp.float32)).astype(x.dtype)

        mixed = jnp.concatenate([gdn_out, pool_out], axis=-1) @ w_out[l]
        x = x + mod[:, 5][:, None, :] * mixed

        h = modulate(rms_norm(x, norm_ffn2[l]), mod[:, 6], mod[:, 7])
        x = x + 0.5 * mod[:, 8][:, None, :] * swiglu(h, ffn2_gate[l], ffn2_up[l], ffn2_down[l])
    return rms_norm(x, final_norm)
```

```python
import os
import numpy as np
import ml_dtypes
from contextlib import ExitStack
import concourse.bass as bass
import concourse.mybir as mybir
from concourse.bass_utils import run_bass_kernel_spmd

F32 = mybir.dt.float32
BF16 = mybir.dt.bfloat16
AF = mybir.ActivationFunctionType
ALU = mybir.AluOpType

D = 1024
KC = 8
H = 4
DH = 128
D_IN = 2568
EPS = 1e-6
NEG = -30000.0
NSLOT = 3
ENGS = ("pe", "act", "dve", "pool", "sp")


class Sched:
    def __init__(self):
        self.ops = []
        self.last_w = {}
        self.readers = {}
        self.eng_ops = {e: [] for e in ENGS}
        self.extra = ("ARENA",)

    def add(self, eng, fn, reads=(), writes=(), chan=None):
        idx = len(self.ops)
        if "ARENA" not in writes and "ARENA2" not in writes and "ARENA3" not in writes:
            reads = tuple(reads) + self.extra
        deps = set()
        for r in reads:
            w = self.last_w.get(r)
            if w is not None:
                deps.add(w)
        for w_ in writes:
            w = self.last_w.get(w_)
            if w is not None:
                deps.add(w)
            deps.update(self.readers.get(w_, ()))
        self.ops.append(dict(eng=eng, fn=fn, deps=sorted(deps, reverse=True), chan=chan,
                             ord=len(self.eng_ops[eng])))
        self.eng_ops[eng].append(idx)
        for r in reads:
            self.readers.setdefault(r, []).append(idx)
        for w_ in writes:
            self.last_w[w_] = idx
            self.readers[w_] = []
        return idx

    def plan(self):
        know = {e: {} for e in ENGS}
        chan_cnt = {}
        self.milestones = {e: set() for e in ENGS}
        for op in self.ops:
            e = op["eng"]
            waits = {}
            kn = know[e]
            for d in op["deps"]:
                P = self.ops[d]
                key, val = P["token"]
                if P["chan"] is None and P["eng"] == "pe" and e == "pe":
                    continue
                if kn.get(key, 0) >= val:
                    continue
                waits[key] = max(waits.get(key, 0), val)
                for k2, v2 in P["snap"].items():
                    if kn.get(k2, 0) < v2:
                        kn[k2] = v2
                kn[key] = max(kn.get(key, 0), val)
            op["waits"] = waits
            for (kind, name), val in waits.items():
                if kind == "E":
                    self.milestones[name].add(val)
            if op["chan"] is not None:
                c = op["chan"]
                if c == "init":
                    op["token"] = (("C", c), 1)
                else:
                    chan_cnt[c] = chan_cnt.get(c, 0) + 1
                    op["token"] = (("C", c), chan_cnt[c])
                op["snap"] = dict(kn)
            else:
                op["token"] = (("E", e), op["ord"] + 1)
                snap = dict(kn)
                snap[("E", e)] = op["ord"] + 1
                op["snap"] = snap
        self.chan_cnt = chan_cnt

    def emit(self, nc, es):
        chans = set(op["chan"] for op in self.ops if op["chan"] is not None)
        sem_e = {e: es.enter_context(nc.semaphore("e_" + e)) for e in ENGS}
        sem_c = {c: es.enter_context(nc.semaphore("c_" + c)) for c in sorted(chans)}
        n_init = sum(1 for op in self.ops if op["chan"] == "init")
        ms_count = {}
        for e in ENGS:
            for i, v in enumerate(sorted(self.milestones[e])):
                ms_count[(e, v)] = i + 1
        block = es.enter_context(nc.Block())

        def run(ename, eng):
            for idx in self.eng_ops[ename]:
                op = self.ops[idx]
                for (kind, name), val in op["waits"].items():
                    if kind == "E":
                        eng.wait_ge(sem_e[name], ms_count[(name, val)])
                    elif name == "init":
                        eng.wait_ge(sem_c[name], 16 * n_init)
                    else:
                        eng.wait_ge(sem_c[name], 16 * val)
                if op["fn"] is None:
                    continue
                r = op["fn"](eng)
                last = r[-1] if isinstance(r, (list, tuple)) else r
                if op["chan"] is not None:
                    last.then_inc(sem_c[op["chan"]], 16)
                elif (op["ord"] + 1) in self.milestones[ename]:
                    last.then_inc(sem_e[ename], 1)

        @block.sync
        def _(e):
            run("sp", e)

        @block.tensor
        def _(e):
            run("pe", e)

        @block.scalar
        def _(e):
            run("act", e)

        @block.vector
        def _(e):
            run("dve", e)

        @block.gpsimd
        def _(e):
            run("pool", e)


def build_program(T=8192, TT=512, DFF=2816):
    NST = TT // 128
    NT = T // TT
    NFC = DFF // 128
    NGU = DFF // 256
    NWDB = (NFC + 7) // 8
    assert T % TT == 0 and DFF % 256 == 0 and TT % 128 == 0
    nc = bass.Bass("TRN2", target_bir_lowering=False)
    SB = [Sched()]

    def din(name, shape, dt=F32):
        return nc.dram_tensor(name, list(shape), dt, kind="ExternalInput").ap()

    x_d = din("x", [T, D])
    out_d = nc.dram_tensor("out", [T, D], F32, kind="ExternalOutput").ap()
    ccol_d = din("c_col", [128, KC])
    wada_d = din("w_ada", [D, 9 * D])
    bada_d = din("b_ada", [1, 9 * D])
    gains_d = din("gains_col", [128, 3 * KC])
    final_d = din("final_norm", [1, D])
    fg_d = [din("f1g", [D, DFF]), din("f2g", [D, DFF])]
    fu_d = [din("f1u", [D, DFF]), din("f2u", [D, DFF])]
    fd_d = [din("f1d", [DFF, D]), din("f2d", [DFF, D])]
    win_d = din("w_in", [D, D_IN])
    convw_d = din("convw_col", [128, 12 * 4])
    alog_d = din("a_log", [1, H])
    dtb_d = din("dt_bias", [1, H])
    gn_d = din("gdn_norm", [1, DH])
    poolw_d = din("pool_w", [4, 128, 128])
    pscale_d = din("pscale_col", [128, 4])
    wout_d = din("w_out", [D, D])
    ident_d = din("ident", [128, 128], BF16)
    u32_d = din("u32", [128, 128])
    negmask_d = din("negmask", [128, 128])
    strict_d = din("strict01", [128, 128], BF16)
    invcnt_d = din("invcnt0", [128, 4 * 16])
    masks_d = din("masks", [128, 6 * 128], BF16)

    blocks = []

    def addblk(halves):
        blocks.append(halves)
        return len(blocks) - 1

    GU = [[None] * NGU for _ in range(2)]
    WD = [[[None] * NWDB for _ in range(2)] for _ in range(2)]
    for f in range(2):
        gsrc = fg_d[f].rearrange("(k p) c -> p k c", p=128)
        usrc = fu_d[f].rearrange("(k p) c -> p k c", p=128)
        dsrc = fd_d[f].rearrange("(f p) c -> p f c", p=128)
        for g in range(NGU):
            GU[f][g] = addblk([(gsrc[:, :, g * 256:(g + 1) * 256], (KC, 256)),
                               (usrc[:, :, g * 256:(g + 1) * 256], (KC, 256))])
        for hf in range(2):
            for b in range(NWDB):
                hs = []
                for q in range(2):
                    f0 = b * 8 + q * 4
                    nf = min(4, NFC - f0)
                    if nf > 0:
                        hs.append((dsrc[:, f0:f0 + nf, hf * 512:(hf + 1) * 512], (nf, 512)))
                WD[f][hf][b] = addblk(hs)
    wsrc = win_d.rearrange("(k p) c -> p k c", p=128)
    WIN = {}
    for name, c0 in (("Q", 0), ("K", 512), ("V", 1024), ("Z", 1536), ("P", 2056)):
        WIN[name] = addblk([(wsrc[:, 0:4, c0:c0 + 512], (4, 512)), (wsrc[:, 4:8, c0:c0 + 512], (4, 512))])
    WIN["AB"] = addblk([(wsrc[:, :, 2048:2056], (KC, 8))])
    osrc = wout_d.rearrange("(r p) c -> p r c", p=128)
    WO = [addblk([(osrc[:, 4 * b + 2 * q:4 * b + 2 * q + 2, :], (2, 1024)) for q in range(2)]) for b in range(2)]
    NBLK = len(blocks)
    scr = nc.dram_tensor("wscr", [NBLK, 128, 4096], BF16, kind="Internal").ap()

    def tile_seq():
        s = []
        for f in (0, 1):
            ff = [GU[f][g] for g in range(NGU)]
            for hf in range(2):
                ff += [WD[f][hf][b] for b in range(NWDB)]
            if f == 0:
                s += ff
                s += [WIN["Q"], WIN["K"], WIN["V"], WIN["AB"], WIN["P"], WIN["Z"], WO[0], WO[1]]
            else:
                s += ff
        return s

    seq1 = tile_seq()
    gseq = seq1 * NT

    es = ExitStack()
    with es:
        def sb(name, shape, dt=F32):
            return es.enter_context(nc.sbuf_tensor(name, list(shape), dt))

        Xb = [sb("X0", [128, NST, D]), sb("X1", [128, NST, D])]
        cx = [0]
        xhat = sb("xhat", [128, NST, D], BF16)
        junk2 = sb("junk2", [128, H, 128], BF16)
        hT = sb("hT", [128, KC, TT], BF16)
        ring = [sb("ring%d" % i, [128, 4096], BF16) for i in range(NSLOT)]
        dtmp = [sb("dtmp0", [128, 512])] * 2
        gate_bc = [sb("gate%d" % i, [128, D]) for i in range(3)]
        final_bc = sb("final_bc", [128, D])
        ident = sb("ident_s", [128, 128], BF16)
        u32 = sb("u32_s", [128, 128])
        ones32 = sb("ones32", [128, 128])
        negones32 = sb("negones32", [128, 128])
        negmask = sb("negmask_s", [128, 128])
        strict01 = sb("strict_s", [128, 128], BF16)
        invcnt0 = sb("invcnt_s", [128, 4, 16])
        masks = sb("masks_s", [128, 6, 128], BF16)
        ccol = sb("ccol", [128, KC])
        scol = sb("scol", [128, KC])
        gains = sb("gains", [128, 3, KC])
        modcol = sb("modcol", [128, 9 * KC])
        g1 = sb("g1", [128, 3, KC])
        convw = sb("convw", [128, 12, 4])
        alog_bc = sb("alog_bc", [128, H])
        negA = sb("negA", [128, H])
        dtb_bc = sb("dtb_bc", [128, H])
        gn_bc = sb("gn_bc", [128, DH])
        poolw16 = sb("poolw16", [128, 4, 128], BF16)
        pscale = sb("pscale", [128, 4])
        ss = sb("ss", [128, NST])
        rstd = sb("rstd", [128, NST])
        hist = sb("hist", [128, 12, 3])
        phist = sb("phist", [128, 4, 16])
        S32 = [sb("S32_%d" % i, [128, H, 128]) for i in range(2)]
        S16 = sb("S16", [128, H, 128], BF16)
        abraw = sb("abraw", [128, NST, 8])
        beta = sb("beta", [128, NST, H])
        gt = sb("gt", [128, NST, H])
        gtmp = [sb("gtmp%d" % i, [128, NST, H]) for i in range(3)]
        gcs = sb("gcs", [128, NST, 8])
        egc = sb("egc", [128, NST, H])
        negegc = sb("negegc", [128, NST, H])
        edl = sb("edl", [128, NST, H])
        cdec = sb("cdec", [128, NST, H])
        oss = sb("oss", [128, H])
        orstd = sb("orstd", [128, H])
        one11 = sb("one11", [1, 1])
        ARENA_F32 = 26240
        arena = sb("arena", [128, ARENA_F32])
        apos = [0]

        def carve(nelem, dt=F32, shape=None):
            n32 = nelem if dt == F32 else (nelem + 1) // 2
            a = apos[0]
            apos[0] += n32
            assert apos[0] <= ARENA_F32, "arena overflow"
            v = arena[:, a:a + n32]
            if dt != F32:
                v = v.bitcast(dt)
            if shape is not None:
                names = " ".join("d%d" % i for i in range(len(shape)))
                kw = {"d%d" % i: shape[i] for i in range(1, len(shape))}
                v = v.rearrange("p (%s) -> p %s" % (names, names), **kw)
            return v

        apos[0] = 0
        stage = [carve(2048) for _ in range(4)]
        rowb = [carve(256)[0:1, :] for _ in range(2)]
        badab = [carve(256)[0:1, :] for _ in range(2)]
        poolw32 = carve(512, F32, (4, 128))
        apos[0] = 0
        sg = [carve(TT) for _ in range(2)]
        actT = carve(NFC * TT, BF16, (NFC, TT))
        ffn_end = apos[0]
        apos[0] = 0
        NCS = 3
        pc = [carve(3 + TT) for _ in range(NCS)]
        cacc = [carve(TT) for _ in range(NCS)]
        csq = [carve(TT) for _ in range(NCS)]
        pw = [carve(16 + TT) for _ in range(2)]
        wbuf = [carve(16 + TT) for _ in range(2)]
        ptmp16 = carve(16)
        pooledT = [carve(TT, BF16) for _ in range(2)]
        apos[0] = max(apos[0], ffn_end)
        QT = carve(H * TT, BF16, (H, TT))
        KT = carve(H * TT, BF16, (H, TT))
        VT = carve(H * TT, BF16, (H, TT))
        zsil = [carve(512) for _ in range(NST)]
        catT = carve(KC * TT, BF16, (KC, TT))
        backonly_start = apos[0]
        otok = [carve(512) for _ in range(2)]
        ontmp = carve(512)
        og = [carve(512, BF16) for _ in range(2)]
        gU = carve(H * 128, F32, (H, 128))
        ebufs = [carve(H * 128, F32, (H, 128)) for _ in range(2)]
        NCH = 2
        Xp = [[carve(H * 128, BF16, (H, 128)) for _ in range(2)] for _ in range(NCH)]
        Np = [[carve(H * 128, BF16, (H, 128)) for _ in range(2)] for _ in range(NCH)]
        Xf = [carve(H * 128, BF16, (H, 128)) for _ in range(NCH)]
        Nf = [carve(H * 128, BF16, (H, 128)) for _ in range(NCH)]
        RT = [carve(H * 128, BF16, (H, 128)) for _ in range(NST)]
        intraT = [carve(H * 128, BF16, (H, 128)) for _ in range(NST)]
        Vtok = [carve(H * 128, BF16, (H, 128)) for _ in range(NST)]
        kd = [carve(H * 128, BF16, (H, 128)) for _ in range(NST)]
        dbuf = [carve(H * 128, BF16, (H, 128))] * 2
        vnew = [carve(H * 128, BF16, (H, 128))] * 2
        qs = [carve(H * 128, F32, (H, 128))] * 2
        end_all = apos[0]
        apos[0] = backonly_start
        sgB = [carve(TT) for _ in range(2)]
        actTB = carve(NFC * TT, BF16, (NFC, TT))
        assert apos[0] <= end_all, "FFN-B views exceed the back-only region"
        apos[0] = end_all
        print("arena floats used", apos[0], "of", ARENA_F32, "sbuf remaining", nc.sbuf_bytes_remaining)

        ps = [es.enter_context(nc.psum_tensor("ps%d" % i, [128, 512], F32)) for i in range(8)]
        rr = [0]

        mix8 = [False]

        def nb():
            b = (rr[0] % 8) if mix8[0] else 4 + (rr[0] % 4)
            rr[0] += 1
            return b

        def pipeline_rounds(chains, skew):
            total = max([i * skew + len(ch) for i, ch in enumerate(chains)] + [0])
            rounds = []
            for tau in range(total):
                fs = []
                for i, ch in enumerate(chains):
                    k = tau - i * skew
                    if 0 <= k < len(ch):
                        fs.append(ch[k])
                rounds.append(fs)
            return [(lambda fs_=fs_: [f_() for f_ in fs_]) for fs_ in rounds]

        def pipeline(chains, skew, tag='1'):
            for r_ in pipeline_rounds(chains, skew):
                r_()

        def psb(b):
            return ps[b][:].bitcast(BF16)

        def dma(out, in_, reads, writes, chan, q="sp"):
            return SB[0].add(q, lambda e: e.dma_start(out=out, in_=in_), reads, writes, chan=chan)

        def act(out, in_, func, reads, writes, bias=None, scale=None, accum_out=None):
            kw = {}
            if bias is not None:
                kw["bias"] = bias
            if scale is not None:
                kw["scale"] = scale
            if accum_out is not None:
                kw["accum_out"] = accum_out
            return SB[0].add("act", lambda e: e.activation(out=out, in_=in_, func=func, **kw), reads, writes)

        def tt(eng, out, in0, in1, op, reads, writes):
            return SB[0].add(eng, lambda e: e.tensor_tensor(out=out, in0=in0, in1=in1, op=op), reads, writes)

        def ts(eng, out, in0, s1, s2, op0, op1, reads, writes):
            if op1 is None:
                return SB[0].add(eng, lambda e: e.tensor_scalar(out=out, in0=in0, scalar1=s1, scalar2=None, op0=op0),
                             reads, writes)
            return SB[0].add(eng, lambda e: e.tensor_scalar(out=out, in0=in0, scalar1=s1, scalar2=s2, op0=op0, op1=op1),
                         reads, writes)

        def stt(eng, out, in0, scalar, in1, op0, op1, reads, writes):
            return SB[0].add(eng, lambda e: e.scalar_tensor_tensor(out=out, in0=in0, scalar=scalar, in1=in1,
                                                               op0=op0, op1=op1), reads, writes)

        def cp(eng, out, in_, reads, writes):
            if eng == "act":
                return SB[0].add("act", lambda e: e.copy(out=out, in_=in_), reads, writes)
            return SB[0].add(eng, lambda e: e.tensor_copy(out=out, in_=in_), reads, writes)

        def mms(items, reads, writes):
            def fn(e):
                r = []
                for (o, l, rh, st, sp_) in items:
                    r.append(e.matmul(o, lhsT=l, rhs=rh, start=st, stop=sp_))
                return r
            return SB[0].add("pe", fn, reads, writes)

        def tps(items, reads, writes):
            def fn(e):
                return [e.transpose(o, i, ident[:]) for (o, i) in items]
            return SB[0].add("pe", fn, reads, writes)

        def memset(eng, ap, val, writes):
            return SB[0].add(eng, lambda e: e.memset(ap, val), (), writes)

        bscr = sb("bscr", [128, 1])
        bscr2 = sb("bscr2", [128, 1])
        bscr3 = sb("bscr3", [128, 1])

        def barrier(eng="dve"):
            SB[0].add(eng, lambda e: e.memset(bscr[:], 0.0), (), ("ARENA",))

        AR = ()

        def barrier2():
            SB[0].add("dve", lambda e: e.memset(bscr2[:], 0.0), (), ("ARENA2", "bscr2"))

        class tok2:
            def __enter__(self_):
                self_.old = SB[0].extra
                SB[0].extra = ("ARENA", "ARENA2")

            def __exit__(self_, *a):
                SB[0].extra = self_.old

        def barrier3():
            SB[0].add("dve", lambda e: e.memset(bscr3[:], 0.0), (), ("ARENA3", "bscr3"))

        def in_tok3(fn):
            def g():
                old = SB[0].extra
                SB[0].extra = ("ARENA", "ARENA3")
                fn()
                SB[0].extra = old
            return g

        def in_tok2(fn):
            def g():
                with tok2():
                    fn()
            return g

        def with_x(xb, fn):
            def g():
                old = cx[0]
                cx[0] = xb
                fn()
                cx[0] = old
            return g

        def ld(out, in_, w):
            dma(out, in_, (), (w,), "init")

        ld(ident[:], ident_d, "ident")
        ld(u32[:], u32_d, "u32")
        ld(negmask[:], negmask_d, "negmask")
        ld(strict01[:], strict_d, "strict01")
        ld(invcnt0[:].rearrange("p g c -> p (g c)"), invcnt_d, "invcnt0")
        ld(masks[:].rearrange("p g c -> p (g c)"), masks_d, "masks")
        ld(ccol[:], ccol_d, "ccol")
        ld(gains[:].rearrange("p s k -> p (s k)"), gains_d, "gains")
        ld(final_bc[:], final_d.partition_broadcast(128), "final_bc")
        ld(convw[:].rearrange("p c k -> p (c k)"), convw_d, "convw")
        ld(alog_bc[:], alog_d.partition_broadcast(128), "alog")
        ld(dtb_bc[:], dtb_d.partition_broadcast(128), "dtb")
        ld(gn_bc[:], gn_d.partition_broadcast(128), "gn")
        ld(poolw32[:], poolw_d.rearrange("g i o -> i g o"), "poolw32")
        ld(pscale[:], pscale_d, "pscale")

        memset("dve", ones32[:], 1.0, ("ones32",))
        memset("dve", negones32[:], -1.0, ("negones32",))
        memset("dve", one11[:], 1.0, ("one11",))
        for i in range(2):
            memset("pool", S32[i][:], 0.0, tuple("S32_%d_h%d" % (i, h_) for h_ in range(H)))
        memset("pool", S16[:], 0.0, tuple("S16_h%d" % h_ for h_ in range(H)))
        memset("pool", hist[:], 0.0, ("hist",))
        memset("pool", phist[:], 0.0, ("phist",))
        for i in range(NSLOT):
            memset("pool", ring[i][:], 0.0, ("w%d" % i,))
        cp("dve", poolw16[:], poolw32[:], ("poolw32",), ("poolw16",))
        act(negA[:], alog_bc[:], AF.Exp, ("alog",), ("negA",))
        ts("dve", negA[:], negA[:], -1.0, None, ALU.mult, None, ("negA",), ("negA",))
        act(scol[:], ccol[:], AF.Silu, ("ccol",), ("scol",))

        wada_v = wada_d.rearrange("(k p) c -> p k c", p=128)
        colbank = 7
        NCB = 9 * D // 256
        for cb in range(NCB):
            stg = stage[cb % 4]
            sname = "stage%d" % (cb % 4)
            sv = stg.rearrange("p (k c) -> p k c", c=256)
            dma(sv, wada_v[:, :, cb * 256:(cb + 1) * 256], (), (sname,), "stg%d" % (cb % 4))
            bb = badab[cb % 2]
            dma(bb[:], bada_d[:, cb * 256:(cb + 1) * 256], (), ("badab%d" % (cb % 2),), "bada%d" % (cb % 2))
            bk = 4 + (cb % 2)
            mms([(ps[bk][0:1, 0:256], scol[:, k:k + 1], sv[:, k, :], k == 0, k == KC - 1) for k in range(KC)],
                (sname, "scol"), ("ps%d" % bk,))
            rb = rowb[cb % 2]
            rn = "rowb%d" % (cb % 2)
            tt("dve", rb[:], ps[bk][0:1, 0:256], bb[:], ALU.add, ("ps%d" % bk, "badab%d" % (cb % 2)), (rn,))
            mms([(ps[colbank][:, 2 * cb + j:2 * cb + j + 1], rb[0:1, j * 128:(j + 1) * 128], one11[0:1, 0:1], True, True)
                 for j in range(2)], (rn, "one11"), ("pscol",))
            m = (cb * 256) // D
            if m in (2, 5, 8):
                gi = m // 3
                c0 = (cb * 256) % D
                bk2 = 6
                mms([(ps[bk2][:, 0:256], ones32[0:1, :], rb[0:1, :], True, True)], (rn, "ones32"), ("ps6",))
                act(gate_bc[gi][:, c0:c0 + 256], ps[bk2][:, 0:256], AF.Copy, ("ps6",), ("gate%d" % gi,),
                    scale=(1.0 if gi == 1 else 0.5))
        cp("dve", modcol[:], ps[colbank][:, 0:9 * KC], ("pscol",), ("modcol",))
        for s_ in range(3):
            stt("dve", g1[:, s_, :], modcol[:, (3 * s_ + 1) * KC:(3 * s_ + 2) * KC], 1.0, gains[:, s_, :],
                ALU.add, ALU.mult, ("modcol", "gains"), ("g1",))

        cast_engs = ("act", "dve")
        hcount = [0]
        for blk in range(NBLK):
            slot = blk % NSLOT
            wn = "w%d" % slot
            for q, (src, (a_, b_)) in enumerate(blocks[blk]):
                si = hcount[0] % 4
                hcount[0] += 1
                n = a_ * b_
                sv = stage[si][:, 0:n].rearrange("p (a b) -> p a b", b=b_)
                dma(sv, src, (), ("stage%d" % si,), "stg%d" % si)
                cp(cast_engs[hcount[0] % 2], ring[slot][:, q * 2048:q * 2048 + n], stage[si][:, 0:n],
                   ("stage%d" % si,), (wn,))
            dma(scr[blk], ring[slot][:], (wn,), ("scr%d" % blk,), "wst%d" % slot)
        barrier("dve")

        emitted = [0]
        recording = [False]
        rec_seq = []

        def acquire(i, oldest=None):
            if recording[0]:
                return i % NSLOT
            lim = min((i if oldest is None else oldest) + NSLOT - 1, len(gseq) - 1)
            while emitted[0] <= lim:
                j = emitted[0]
                sl = j % NSLOT
                dma(ring[sl][:], scr[gseq[j]], ("scr%d" % gseq[j],), ("w%d" % sl,), "wld%d" % sl)
                emitted[0] += 1
            return i % NSLOT

        gi_ctr = [0]
        held = []

        def next_block(expect, oldest=None):
            i = gi_ctr[0]
            if recording[0]:
                rec_seq.append(expect)
            else:
                assert gseq[i] == expect, (i, gseq[i], expect)
            gi_ctr[0] += 1
            return acquire(i, oldest), i

        def xres(st, hf=None):
            if hf is None:
                return ("x%d_%d_0" % (cx[0], st), "x%d_%d_1" % (cx[0], st))
            return ("x%d_%d_%d" % (cx[0], st, hf),)

        def norm_to_hT(s_):
            for st in range(NST):
                act(xhat[:, st, :], Xb[cx[0]][:, st, :], AF.Square, xres(st), ("xhat%d" % st, "ss%d" % st),
                    accum_out=ss[:, st:st + 1])
            ts("dve", rstd[:], ss[:], 1.0 / D, EPS, ALU.mult, ALU.add, tuple("ss%d" % st_ for st_ in range(NST)), ("rstd",))
            act(rstd[:], rstd[:], AF.Sqrt, ("rstd",), ("rstd",))
            SB[0].add("dve", lambda e: e.reciprocal(out=rstd[:], in_=rstd[:]), ("rstd",), ("rstd",))
            for st in range(NST):
                if st % 2 == 0:
                    ts("dve", xhat[:, st, :], Xb[cx[0]][:, st, :], rstd[:, st:st + 1], None, ALU.mult, None,
                       xres(st) + ("rstd",), ("xhat%d" % st,))
                else:
                    act(xhat[:, st, :], Xb[cx[0]][:, st, :], AF.Copy, xres(st) + ("rstd",), ("xhat%d" % st,),
                        scale=rstd[:, st:st + 1])
            for k in range(KC):
                b = nb()
                pv = psb(b)
                tps([(pv[:, st * 128:(st + 1) * 128], xhat[:, st, k * 128:(k + 1) * 128]) for st in range(NST)],
                    tuple("xhat%d" % st for st in range(NST)) + ("ident",), ("ps%d" % b,))
                if k % 2 == 0:
                    act(hT[:, k, :], pv[:, 0:TT], AF.Identity, ("ps%d" % b, "g1", "modcol"), ("hT%d" % k,),
                        bias=modcol[:, 3 * s_ * KC + k:3 * s_ * KC + k + 1], scale=g1[:, s_, k:k + 1])
                else:
                    ts("dve", hT[:, k, :], pv[:, 0:TT], g1[:, s_, k:k + 1],
                       modcol[:, 3 * s_ * KC + k:3 * s_ * KC + k + 1], ALU.mult, ALU.add,
                       ("ps%d" % b, "g1", "modcol"), ("hT%d" % k,))

        HT_ALL = tuple("hT%d" % k for k in range(KC))

        def resid_add(b, st, hf, gi, k):
            d_ = dtmp[k % 2]
            dn = "dtmp0"
            tt("dve", d_[:], ps[b][:], gate_bc[gi][:, hf * 512:(hf + 1) * 512], ALU.mult,
               ("ps%d" % b, "gate%d" % gi), (dn,))
            xs = Xb[cx[0]][:, st, hf * 512:(hf + 1) * 512]
            tt("pool", xs, xs, d_[:], ALU.add, (dn,) + xres(st, hf), xres(st, hf))

        def ffn_units(f, s_, overlapped, vset="A"):
            sg_, actT_ = (sg, actT) if vset == "A" else (sgB, actTB)
            sgn, actn = ("sg%d", "actT%d") if vset == "A" else ("sgB%d", "actTB%d")
            units = [lambda: norm_to_hT(s_)]
            st_ = {}

            def gu_unit(g, j):
                def u():
                    if j == 0:
                        slot, _ = next_block(GU[f][g])
                        st_["gu"] = slot
                    slot = st_["gu"]
                    W = ring[slot][:].rearrange("p (w k c) -> p w k c", w=2, k=KC)
                    wn = "w%d" % slot
                    fc = 2 * g + j
                    bg = fc % 2
                    bu = 2 + fc % 2
                    mms([(ps[bg][:, 0:TT], W[:, 0, k, j * 128:(j + 1) * 128], hT[:, k, :], k == 0, k == KC - 1)
                         for k in range(KC)], (wn,) + HT_ALL, ("ps%d" % bg,))
                    mms([(ps[bu][:, 0:TT], W[:, 1, k, j * 128:(j + 1) * 128], hT[:, k, :], k == 0, k == KC - 1)
                         for k in range(KC)], (wn,) + HT_ALL, ("ps%d" % bu,))
                    act(sg_[fc % 2][:], ps[bg][:, 0:TT], AF.Silu, ("ps%d" % bg,), (sgn % (fc % 2),))
                    tt("dve", actT_[:, fc, :], sg_[fc % 2][:], ps[bu][:, 0:TT], ALU.mult,
                       (sgn % (fc % 2), "ps%d" % bu), (actn % fc,))
                return u
            for g in range(NGU):
                for j in range(2):
                    units.append(gu_unit(g, j))

            def wd_unit(hf, b, st):
                base = 0 if (overlapped or hf == 1) else 4

                def u():
                    if st == 0:
                        slot, _ = next_block(WD[f][hf][b], oldest=(held[0] if held else None))
                        st_["wd"] = slot
                    slot = st_["wd"]
                    W = ring[slot][:].rearrange("p (f c) -> p f c", c=512)
                    wn = "w%d" % slot
                    nf = min(8, NFC - b * 8)
                    mms([(ps[base + st][:], actT_[:, b * 8 + q, st * 128:(st + 1) * 128], W[:, q, :],
                          (b == 0 and q == 0), (b == NWDB - 1 and q == nf - 1)) for q in range(nf)],
                        (wn,) + tuple(actn % (b * 8 + q) for q in range(nf)), ("ps%d" % (base + st),))
                return u

            def res_unit(hf, st):
                base = 0 if (overlapped or hf == 1) else 4
                return lambda: resid_add(base + st, st, hf, (0 if s_ == 0 else 2), 0)
            for hf in range(2):
                for b in range(NWDB):
                    for st in range(NST):
                        units.append(wd_unit(hf, b, st))
                for st in range(NST):
                    units.append(res_unit(hf, st))
            n_gu = 1 + 2 * NGU
            wrap = in_tok2 if vset == "A" else in_tok3
            dn_ = [wrap(u) for u in units[n_gu:]]
            groups = []
            p_ = 0
            for hf in range(2):
                for b in range(NWDB):
                    g_ = dn_[p_:p_ + NST]
                    p_ += NST
                    if b == NWDB - 1:
                        g_ = g_ + dn_[p_:p_ + NST]
                        p_ += NST
                    groups.append(g_)
            assert p_ == len(dn_)
            return [wrap(u) for u in units[:n_gu]], groups

        def mixer(it):
            FRONT_OVL = os.environ.get("K_FRONT_OVL", "1") == "1"
            mix8[0] = not FRONT_OVL
            norm_to_hT(1)
            hsl = lambda st: slice(st * 128, (st + 1) * 128)
            blkW = {}

            def get_w(name, c=512):
                if name not in blkW:
                    slot, _ = next_block(WIN[name])
                    if name == "AB":
                        blkW[name] = (ring[slot][:, 0:64].rearrange("p (k c) -> p k c", c=8), "w%d" % slot)
                    else:
                        blkW[name] = (ring[slot][:].rearrange("p (k c) -> p k c", c=512), "w%d" % slot)
                return blkW[name]

            def conv_chain(gi_, name, h):
                c = gi_ * 4 + h
                i = c % NCS
                p_, ca, cs = pc[i], cacc[i], csq[i]
                pn, can, csn = "pc%d" % i, "cacc%d" % i, "csq%d" % i

                def s0():
                    W, wn = get_w(name)
                    b = nb()
                    mms([(ps[b][:, 0:TT], W[:, k, h * 128:(h + 1) * 128], hT[:, k, :], k == 0, k == KC - 1)
                         for k in range(KC)], (wn,) + HT_ALL, ("ps%d" % b,))
                    cp("pool", p_[:, 0:3], hist[:, c, :], ("hist%d" % c, "hist"), (pn + "h",))
                    act(p_[:, 3:3 + TT], ps[b][:, 0:TT], AF.Copy, ("ps%d" % b,), (pn,))
                    cp("pool", hist[:, c, :], p_[:, TT:TT + 3], (pn, pn + "h"), ("hist%d" % c,))

                def s1():
                    act(ca[:], p_[:, 0:TT], AF.Copy, (pn, pn + "h", "convw"), (can,), scale=convw[:, c, 0:1])
                    for k in (1, 2, 3):
                        stt("dve", ca[:], p_[:, k:k + TT], convw[:, c, k:k + 1], ca[:], ALU.mult, ALU.add,
                            (pn, pn + "h", "convw", can), (can,))

                def s2():
                    if name == "V":
                        act(VT[:, h, :], ca[:], AF.Silu, (can,), ("VT%d" % h,))
                    else:
                        act(cs[:], ca[:], AF.Silu, (can,), (csn,))
                        tt("pool", ca[:], cs[:], cs[:], ALU.mult, (csn, can), (can,))

                def s3():
                    b2 = nb()
                    mms([(ps[b2][:, 0:TT], ones32[:], ca[:], True, True)], (can, "ones32"), ("ps%d" % b2,))
                    ts("dve", ca[:], ps[b2][:, 0:TT], EPS, None, ALU.add, None, ("ps%d" % b2, can), (can,))

                def s4():
                    dst = QT if name == "Q" else KT
                    act(ca[:], ca[:], AF.Sqrt, (can,), (can,))
                    SB[0].add("dve", lambda e: e.reciprocal(out=ca[:], in_=ca[:]), (can,), (can,))
                    stt("dve", dst[:, h, :], cs[:], (DH ** -0.5 if name == "Q" else 1.0), ca[:],
                        ALU.mult, ALU.mult, (csn, can), ("%sT%d" % (name, h),))
                return [s0, s1, s2] if name == "V" else [s0, s1, s2, s3, s4]

            t0, t1, t2 = gtmp

            def g0():
                W, wn = get_w("AB")
                for st in range(NST):
                    b = nb()
                    mms([(ps[b][:, 0:8], hT[:, k, hsl(st)], W[:, k, :], k == 0, k == KC - 1) for k in range(KC)],
                        (wn,) + HT_ALL, ("ps%d" % b,))
                    cp("dve", abraw[:, st, :], ps[b][:, 0:8], ("ps%d" % b,), ("abraw",))

            def g1():
                act(beta[:], abraw[:, :, 0:4], AF.Sigmoid, ("abraw",), ("beta",))
                tt("dve", t0[:], abraw[:, :, 4:8], dtb_bc[:].unsqueeze(1).to_broadcast([128, NST, H]), ALU.add,
                   ("abraw", "dtb"), ("gtmp0",))
                ts("dve", t1[:], t0[:], -1.0, None, ALU.mult, None, ("gtmp0",), ("gtmp1",))
                tt("dve", t1[:], t1[:], t0[:], ALU.max, ("gtmp0", "gtmp1"), ("gtmp1",))

            def g2():
                act(t1[:], t1[:], AF.Exp, ("gtmp1",), ("gtmp1",), scale=-1.0)
                ts("dve", t1[:], t1[:], 1.0, None, ALU.add, None, ("gtmp1",), ("gtmp1",))
                ts("dve", t2[:], t0[:], 0.0, None, ALU.max, None, ("gtmp0",), ("gtmp2",))

            def g3():
                act(t1[:], t1[:], AF.Ln, ("gtmp1",), ("gtmp1",))
                tt("dve", t2[:], t2[:], t1[:], ALU.add, ("gtmp1", "gtmp2"), ("gtmp2",))
                tt("dve", gt[:], t2[:], negA[:].unsqueeze(1).to_broadcast([128, NST, H]), ALU.mult,
                   ("gtmp2", "negA"), ("gt",))

            def g4():
                for st in range(NST):
                    b = nb()
                    mms([(ps[b][:, 0:4], u32[:], gt[:, st, :], True, True),
                         (ps[b][:, 4:8], ones32[:], gt[:, st, :], True, True)], ("gt", "u32", "ones32"),
                        ("ps%d" % b,))
                    cp("dve", gcs[:, st, :], ps[b][:, 0:8], ("ps%d" % b,), ("gcs",))

            def g5():
                act(egc[:], gcs[:, :, 0:4], AF.Exp, ("gcs",), ("egc",))
                tt("dve", edl[:], gcs[:, :, 4:8], gcs[:, :, 0:4], ALU.subtract, ("gcs",), ("edl",))
                act(cdec[:], gcs[:, :, 4:8], AF.Exp, ("gcs",), ("cdec",))
                ts("dve", negegc[:], egc[:], -1.0, None, ALU.mult, None, ("egc",), ("negegc",))
                act(edl[:], edl[:], AF.Exp, ("edl",), ("edl",))

            def pool_chain(g):
                win = 2 ** (g + 1)
                p_ = pw[g % 2]
                pn = "pw%d" % (g % 2)
                po = pooledT[g % 2]
                pon = "pooledT%d" % (g % 2)

                def p0():
                    W, wn = get_w("P")
                    b = nb()
                    mms([(ps[b][:, 0:TT], W[:, k, g * 128:(g + 1) * 128], hT[:, k, :], k == 0, k == KC - 1)
                         for k in range(KC)], (wn,) + HT_ALL, ("ps%d" % b,))
                    cp("pool", p_[:, 0:16], phist[:, g, :], ("phist%d" % g, "phist"), (pn + "h",))
                    act(p_[:, 16:16 + TT], ps[b][:, 0:TT], AF.Copy, ("ps%d" % b,), (pn,))
                    cp("pool", phist[:, g, :], p_[:, TT:TT + 16], (pn, pn + "h"), ("phist%d" % g,))

                def p1():
                    src = p_
                    srcn = (pn, pn + "h")
                    for k in range(g + 1):
                        sh = 2 ** k
                        lo = 2 ** (k + 1)
                        dstb = wbuf[k % 2]
                        dn = "wbuf%d" % (k % 2)
                        tt("pool", dstb[:, lo:16 + TT], src[:, lo:16 + TT],
                           src[:, lo - sh:16 + TT - sh], ALU.add, srcn, (dn,))
                        src = dstb
                        srcn = (dn,)
                    stt("dve", po[:], src[:, 16:16 + TT], 1.0 / win, p_[:, 16:16 + TT], ALU.mult, ALU.subtract,
                        srcn + (pn,), (pon,))
                    if it == 0:
                        tt("dve", ptmp16[:], src[:, 16:32], invcnt0[:, g, :], ALU.mult, srcn + ("invcnt0",),
                           ("ptmp16",))
                        tt("dve", po[:, 0:16], ptmp16[:], p_[:, 16:32], ALU.subtract, ("ptmp16", pn, pon), (pon,))

                def p2():
                    b2 = nb()
                    mms([(ps[b2][:, 0:TT], poolw16[:, g, :], po[:], True, True)], (pon, "poolw16"), ("ps%d" % b2,))
                    act(catT[:, 4 + g, :], ps[b2][:, 0:TT], AF.Copy, ("ps%d" % b2, "pscale"),
                        ("catT%d" % (4 + g),), scale=pscale[:, g:g + 1])
                return [p0, p1, p2]

            chains = [conv_chain(gi_, name, h) for gi_, name in enumerate(("Q", "K", "V")) for h in range(H)]
            chains.append([g0, g1, g2, g3, g4, g5])
            chains += [pool_chain(g) for g in range(4)]

            def z_chain(st):
                def z0():
                    W, wn = get_w("Z")
                    b = nb()
                    mms([(ps[b][:], hT[:, k, hsl(st)], W[:, k, :], k == 0, k == KC - 1) for k in range(KC)],
                        (wn,) + HT_ALL, ("ps%d" % b,))
                    act(zsil[st][:], ps[b][:], AF.Silu, ("ps%d" % b,), ("zsil%d" % st,))
                return [z0]
            chains += [z_chain(st) for st in range(NST)]
            front_rounds = [in_tok2(r_) for r_ in pipeline_rounds(chains, 2)]
            QKV = tuple("%sT%d" % (n_, h) for n_ in "QKV" for h in range(H))

            def intra_steps(st, ch):
                ebuf = ebufs[st % 2]
                en = "ebuf%d" % (st % 2)
                X0 = Xf[ch]
                xn = "Xf%d" % ch

                def i0_():
                    tt("dve", gU[:], u32[:].unsqueeze(1).to_broadcast([128, H, 128]),
                       gt[:, st, :].unsqueeze(2).to_broadcast([128, H, 128]), ALU.mult, ("u32", "gt"), ("gU",))
                    bD = nb()
                    items = []
                    for h in range(H):
                        items.append((ps[bD][:, hsl(h)], ones32[:], gU[:, h, :], True, False))
                        items.append((ps[bD][:, hsl(h)], gU[:, h, :], negones32[:], False, True))
                    mms(items, ("gU", "ones32", "negones32"), ("ps%d" % bD,))
                    tt("dve", ebuf[:], ps[bD][:].rearrange("p (h c) -> p h c", h=H),
                       negmask[:].unsqueeze(1).to_broadcast([128, H, 128]), ALU.add,
                       ("ps%d" % bD, "negmask"), (en,))

                def i1_():
                    act(ebuf[:], ebuf[:], AF.Exp, (en,), (en,))

                def i2_():
                    bK = nb()
                    mms([(ps[bK][:, hsl(h)], KT[:, h, hsl(st)], KT[:, h, hsl(st)], True, True) for h in range(H)],
                        QKV, ("ps%d" % bK,))
                    bQ = nb()
                    mms([(ps[bQ][:, hsl(h)], KT[:, h, hsl(st)], QT[:, h, hsl(st)], True, True) for h in range(H)],
                        QKV, ("ps%d" % bQ,))
                    for h in range(H):
                        stt("dve", X0[:, h, :], ps[bK][:, hsl(h)], beta[:, st, h:h + 1], ebuf[:, h, :],
                            ALU.mult, ALU.mult, ("ps%d" % bK, "beta", en), (xn + "_h%d" % h,))
                    tt("dve", intraT[st][:], ps[bQ][:].rearrange("p (h c) -> p h c", h=H), ebuf[:], ALU.mult,
                       ("ps%d" % bQ, en), ("intraT%d" % st,))

                def i3_():
                    bT = nb()
                    pv = psb(bT)
                    tps([(pv[:, hsl(h)], KT[:, h, hsl(st)]) for h in range(H)], QKV + ("ident",), ("ps%d" % bT,))
                    bV = nb()
                    pvv = psb(bV)
                    tps([(pvv[:, hsl(h)], VT[:, h, hsl(st)]) for h in range(H)], QKV + ("ident",), ("ps%d" % bV,))
                    for h in range(H):
                        act(kd[st][:, h, :], pv[:, hsl(h)], AF.Copy, ("ps%d" % bT, "edl"), ("kd%d_h%d" % (st, h),),
                            scale=edl[:, st, h:h + 1])
                    cp("dve", Vtok[st][:], pvv[:, 0:512].rearrange("p (h c) -> p h c", h=H), ("ps%d" % bV,),
                       ("Vtok%d" % st,))

                def i4_():
                    bN = nb()
                    pv2 = psb(bN)
                    tps([(pv2[:, hsl(h)], X0[:, h, :]) for h in range(H)],
                        tuple(xn + "_h%d" % h_ for h_ in range(H)) + ("ident",), ("ps%d" % bN,))
                    cp("act", Nf[ch][:], pv2[:, 0:512].rearrange("p (h c) -> p h c", h=H), ("ps%d" % bN,),
                       ("Nf%d" % ch,))
                return [i0_, i1_, i2_, i3_, i4_]

            def mask_bc(mi):
                return masks[:, mi, :].unsqueeze(1).to_broadcast([128, H, 128])

            def hview(b):
                return ps[b][:].rearrange("p (h c) -> p h c", h=H)

            def inv_steps(st, ch):
                A1, A2 = Xp[ch]
                B1, B2 = Np[ch]
                a1, a2 = "Xp%d_0" % ch, "Xp%d_1" % ch
                b1, b2 = "Np%d_0" % ch, "Np%d_1" % ch
                xf, nf, rt = tuple("Xf%d_h%d" % (ch, h_) for h_ in range(H)), "Nf%d" % ch, "RT%d" % st
                R = RT[st]
                steps = []

                def base0():
                    tt("pool", A1[:], Xf[ch][:], mask_bc(4), ALU.mult, xf + ("masks",), (a1,))
                    tt("pool", B1[:], Nf[ch][:], mask_bc(5), ALU.mult, (nf, "masks"), (b1,))
                    tt("dve", R[:], ident[:].unsqueeze(1).to_broadcast([128, H, 128]), A1[:], ALU.subtract,
                       (a1, "ident"), (rt,))
                steps.append(base0)

                def level(l):
                    def f():
                        if l % 2 == 1:
                            Xa, Na, Xb, Nb_, xa, na, xb, nb_ = A1, B1, A2, B2, a1, b1, a2, b2
                        else:
                            Xa, Na, Xb, Nb_, xa, na, xb, nb_ = A2, B2, A1, B1, a2, b2, a1, b1
                        p1 = nb()
                        mms([(ps[p1][:, hsl(h)], Xa[:, h, :], Na[:, h, :], True, True) for h in range(H)], (xa, na),
                            ("ps%d" % p1,))
                        cp("act", Nb_[:], hview(p1), ("ps%d" % p1,), (nb_,))
                        if l < 3:
                            p2 = nb()
                            mms([(ps[p2][:, hsl(h)], Na[:, h, :], Xa[:, h, :], True, True) for h in range(H)],
                                (xa, na), ("ps%d" % p2,))
                            cp("dve", Xb[:], hview(p2), ("ps%d" % p2,), (xb,))
                        p3 = nb()
                        mms([(ps[p3][:, hsl(h)], Nb_[:, h, :], R[:, h, :], True, True) for h in range(H)],
                            (nb_, rt), ("ps%d" % p3,))
                        tt("dve", R[:], hview(p3), R[:], ALU.add, ("ps%d" % p3, rt), (rt,))
                    return f
                for l in (1, 2, 3):
                    steps.append(level(l))

                def wtrans():
                    bT = nb()
                    pv = psb(bT)
                    tps([(pv[:, hsl(h)], R[:, h, :]) for h in range(H)], (rt, "ident"), ("ps%d" % bT,))
                    cp("act", A1[:], pv[:, 0:512].rearrange("p (h c) -> p h c", h=H), ("ps%d" % bT,), (a1,))
                steps.append(wtrans)

                def merge(mi, last):
                    def f1():
                        tt("pool", B1[:], Xf[ch][:], mask_bc(mi), ALU.mult, xf + ("masks",), (b1,))
                        tt("pool", B2[:], Nf[ch][:], mask_bc(mi), ALU.mult, (nf, "masks"), (b2,))
                        p1 = nb()
                        mms([(ps[p1][:, hsl(h)], B2[:, h, :], R[:, h, :], True, True) for h in range(H)], (b2, rt),
                            ("ps%d" % p1,))
                        cp("act", A2[:], hview(p1), ("ps%d" % p1,), (a2,))
                        if not last:
                            p2 = nb()
                            mms([(ps[p2][:, hsl(h)], B1[:, h, :], A1[:, h, :], True, True) for h in range(H)],
                                (b1, a1), ("ps%d" % p2,))
                            cp("dve", B1[:], hview(p2), ("ps%d" % p2,), (b1,))

                    def f2():
                        p3 = nb()
                        mms([(ps[p3][:, hsl(h)], A1[:, h, :], A2[:, h, :], True, True) for h in range(H)], (a1, a2),
                            ("ps%d" % p3,))
                        if not last:
                            p4 = nb()
                            mms([(ps[p4][:, hsl(h)], R[:, h, :], B1[:, h, :], True, True) for h in range(H)],
                                (rt, b1), ("ps%d" % p4,))
                        tt("dve", R[:], R[:], hview(p3), ALU.subtract, ("ps%d" % p3, rt), (rt,))
                        if not last:
                            tt("dve", A1[:], A1[:], hview(p4), ALU.subtract, ("ps%d" % p4, a1), (a1,))
                    return [f1, f2]
                steps += merge(1, False) + merge(2, False) + merge(3, True)
                return steps

            back = []
            S16N = tuple("S16_h%d" % h_ for h_ in range(H))
            VNN = tuple("vnew0_h%d" % h_ for h_ in range(H))
            KDN = lambda st_: tuple("kd%d_h%d" % (st_, h_) for h_ in range(H))

            def rec_A(st):
                cur = (it * NST + st) % 2
                Sc, Sn = S32[cur], S32[1 - cur]
                scn, snn = "S32_%d" % cur, "S32_%d" % (1 - cur)
                d_, dn = dbuf[0], "dbuf0"
                q_, qn = qs[0], "qs0"
                v_, vn = vnew[0], "vnew0"

                def a1():
                    bA = nb()
                    mms([(ps[bA][:, hsl(h)], KT[:, h, hsl(st)], S16[:, h, :], True, True) for h in range(H)],
                        QKV + S16N, ("ps%d" % bA,))
                    bB = nb()
                    mms([(ps[bB][:, hsl(h)], QT[:, h, hsl(st)], S16[:, h, :], True, True) for h in range(H)],
                        QKV + S16N, ("ps%d" % bB,))
                    for h in range(H):
                        stt("dve", d_[:, h, :], ps[bA][:, hsl(h)], negegc[:, st, h:h + 1], Vtok[st][:, h, :],
                            ALU.mult, ALU.add, ("ps%d" % bA, "negegc", "Vtok%d" % st), (dn + "_h%d" % h,))
                        act(q_[:, h, :], ps[bB][:, hsl(h)], AF.Copy, ("ps%d" % bB, "egc"), (qn + "_h%d" % h,),
                            scale=egc[:, st, h:h + 1])

                def a2():
                    bC = nb()
                    mms([(ps[bC][:, hsl(h)], RT[st][:, h, :], d_[:, h, :], True, True) for h in range(H)],
                        ("RT%d" % st,) + tuple(dn + "_h%d" % h_ for h_ in range(H)), ("ps%d" % bC,))
                    for h in range(H):
                        ts("dve", v_[:, h, :], ps[bC][:, hsl(h)], beta[:, st, h:h + 1], None, ALU.mult, None,
                           ("ps%d" % bC, "beta"), (vn + "_h%d" % h,))

                def a3():
                    bE = nb()
                    mms([(ps[bE][:, hsl(h)], kd[st][:, h, :], v_[:, h, :], True, True) for h in range(H)],
                        KDN(st) + VNN, ("ps%d" % bE,))
                    bD = nb()
                    mms([(ps[bD][:, hsl(h)], intraT[st][:, h, :], v_[:, h, :], True, True) for h in range(H)],
                        ("intraT%d" % st,) + VNN, ("ps%d" % bD,))
                    for h in range(H):
                        stt("dve", S16[:, h, :], Sc[:, h, :], cdec[:, st, h:h + 1], ps[bE][:, hsl(h)],
                            ALU.mult, ALU.add, (scn + "_h%d" % h, "cdec", "ps%d" % bE), ("S16_h%d" % h,))
                    for h in range(H):
                        stt("dve", Sn[:, h, :], Sc[:, h, :], cdec[:, st, h:h + 1], ps[bE][:, hsl(h)],
                            ALU.mult, ALU.add, (scn + "_h%d" % h, "cdec", "ps%d" % bE), (snn + "_h%d" % h,))
                    o_ = otok[st % 2]
                    on_ = "otok%d" % (st % 2)
                    tt("dve", o_[:], ps[bD][:], q_[:].rearrange("p h c -> p (h c)"), ALU.add,
                       ("ps%d" % bD,) + tuple(qn + "_h%d" % h_ for h_ in range(H)), (on_,))
                return [a1, a2, a3]

            def rec_B(st):
                o_ = otok[st % 2]
                on_ = "otok%d" % (st % 2)
                z_ = zsil[st]
                zn = "zsil%d" % st
                g_ = og[st % 2]
                gn_ = "og%d" % (st % 2)

                def b1():
                    for h in range(H):
                        act(junk2[:, h, :], o_[:, hsl(h)], AF.Square, (on_,), ("junk2_h%d" % h, "oss_h%d" % h),
                            accum_out=oss[:, h:h + 1])
                    ts("dve", orstd[:], oss[:], 1.0 / DH, EPS, ALU.mult, ALU.add,
                       tuple("oss_h%d" % h_ for h_ in range(H)), ("orstd",))
                    act(orstd[:], orstd[:], AF.Sqrt, ("orstd",), ("orstd",))
                    SB[0].add("dve", lambda e: e.reciprocal(out=orstd[:], in_=orstd[:]), ("orstd",), ("orstd",))
                    for h in range(H):
                        stt("dve", ontmp[:, hsl(h)], o_[:, hsl(h)], orstd[:, h:h + 1], gn_bc[:], ALU.mult,
                            ALU.mult, (on_, "orstd", "gn"), ("ontmp_h%d" % h,))
                    tt("dve", g_[:], ontmp[:], z_[:], ALU.mult,
                       tuple("ontmp_h%d" % h_ for h_ in range(H)) + (zn,), (gn_,))

                def b2():
                    bT = nb()
                    pv = psb(bT)
                    tps([(pv[:, hsl(h)], g_[:, hsl(h)]) for h in range(H)], (gn_, "ident"), ("ps%d" % bT,))
                    cp("act", catT[:, 0:4, hsl(st)], pv[:, 0:512].rearrange("p (h c) -> p h c", h=H),
                       ("ps%d" % bT,), tuple("catT%d" % h for h in range(H)))
                return [b1, b2]

            RW = {"a1": 2.0, "a2": 1.0, "a3": 2.0, "b1": 2.0, "b2": 1.0}
            recA = [rec_A(st) for st in range(NST)]
            recB = [rec_B(st) for st in range(NST)]

            def rec_seq(sts):
                out = []
                for st in sts:
                    a_ = recA[st]
                    if st == 0:
                        out += [(a_[0], RW["a1"]), (a_[1], RW["a2"]), (a_[2], RW["a3"])]
                    else:
                        b_ = recB[st - 1]
                        out += [(a_[0], RW["a1"]), (b_[0], RW["b1"]), (a_[1], RW["a2"]), (b_[1], RW["b2"]),
                                (a_[2], RW["a3"])]
                return out

            def merge(xs, ys):
                out = []
                i = j = 0
                while i < len(xs) or j < len(ys):
                    fx = i / max(1, len(xs))
                    fy = j / max(1, len(ys))
                    if j >= len(ys) or (i < len(xs) and fx <= fy):
                        out.append(xs[i])
                        i += 1
                    else:
                        out.append(ys[j])
                        j += 1
                return out

            pairs = [list(range(st0, min(st0 + NCH, NST))) for st0 in range(0, NST, NCH)]
            prev_rec = []
            for sts in pairs:
                pr = [(r_, 1.0) for r_ in pipeline_rounds(
                    [intra_steps(st, st % NCH) + inv_steps(st, st % NCH) for st in sts], 2)]
                back += merge(pr, prev_rec)
                prev_rec = rec_seq(sts)
            back += prev_rec
            b_ = recB[NST - 1]
            back += [(b_[0], RW["b1"]), (b_[1], RW["b2"])]

            def wout():
                s0, i0 = next_block(WO[0])
                s1, _ = next_block(WO[1], oldest=i0)
                Wo = [ring[s0][:].rearrange("p (r c) -> p r c", c=1024),
                      ring[s1][:].rearrange("p (r c) -> p r c", c=1024)]
                for st in range(NST):
                    for hf in range(2):
                        b = nb()
                        mms([(ps[b][:], catT[:, c, hsl(st)], Wo[c // 4][:, c % 4, hf * 512:(hf + 1) * 512],
                              c == 0, c == KC - 1) for c in range(KC)],
                            ("w%d" % s0, "w%d" % s1) + tuple("catT%d" % c for c in range(KC)), ("ps%d" % b,))
                        resid_add(b, st, hf, 1, 0)
            return front_rounds, [(in_tok3(f_), w_) for f_, w_ in back], wout

        xv = x_d.rearrange("(t s p) d -> t s p d", s=NST, p=128)
        ov = out_d.rearrange("(t s p) d -> t s p d", s=NST, p=128)
        def load_x(it, xb):
            old = cx[0]
            cx[0] = xb
            for st in range(NST):
                dma(Xb[xb][:, st, :], xv[it, st], (), xres(st), "xl%d" % st)
            cx[0] = old

        def final_norm_store(it):
            for st in range(NST):
                act(xhat[:, st, :], Xb[cx[0]][:, st, :], AF.Square, xres(st), ("xhat%d" % st, "ss%d" % st),
                    accum_out=ss[:, st:st + 1])
            ts("dve", rstd[:], ss[:], 1.0 / D, EPS, ALU.mult, ALU.add, tuple("ss%d" % st_ for st_ in range(NST)),
               ("rstd",))
            act(rstd[:], rstd[:], AF.Sqrt, ("rstd",), ("rstd",))
            SB[0].add("dve", lambda e: e.reciprocal(out=rstd[:], in_=rstd[:]), ("rstd",), ("rstd",))
            for st in range(NST):
                stt("dve", Xb[cx[0]][:, st, :], Xb[cx[0]][:, st, :], rstd[:, st:st + 1], final_bc[:],
                    ALU.mult, ALU.mult, xres(st) + ("rstd", "final_bc"), xres(st))
                dma(ov[it, st], Xb[cx[0]][:, st, :], xres(st), ("outdram",), "ost%d" % st)

        OVERLAP = os.environ.get("K_OVERLAP", "1") == "1"
        FRONT_OVL_MAIN = os.environ.get("K_FRONT_OVL", "1") == "1"

        def emit_main():
            cx[0] = 0
            load_x(0, 0)
            gu, dn = ffn_units(0, 0, False, "A")
            for u in gu + [u_ for g_ in dn for u_ in g_]:
                u()
            pend = []
            pend_fin = None
            for it in range(NT):
                xb = it % 2
                cx[0] = xb
                barrier2()
                front_rounds, back, wout = mixer(it)
                gidx = 0
                for r_ in front_rounds:
                    before = gi_ctr[0]
                    r_()
                    if gi_ctr[0] > before and gidx < len(pend):
                        held[:] = [gi_ctr[0] - 1]
                        for u in pend[gidx]:
                            u()
                        gidx += 1
                held[:] = []
                for g_ in pend[gidx:]:
                    for u in g_:
                        u()
                if pend_fin is not None:
                    pend_fin()
                mix8[0] = False
                pend = []
                pend_fin = None
                if it + 1 < NT:
                    load_x(it + 1, 1 - xb)
                barrier2()
                barrier3()
                units = []
                if it + 1 < NT:
                    gu, dn = ffn_units(0, 0, OVERLAP, "A")
                    units = [with_x(1 - xb, u) for u in gu + [u_ for g_ in dn for u_ in g_]]
                if OVERLAP:
                    tot_w = max(1e-9, sum(w for _, w in back))
                    ui = 0
                    acc = 0.0
                    for fn, w in back:
                        fn()
                        acc += w * len(units) / tot_w
                        while ui < len(units) and ui < int(acc + 1e-6):
                            units[ui]()
                            ui += 1
                    while ui < len(units):
                        units[ui]()
                        ui += 1
                    wout()
                else:
                    for fn, w in back:
                        fn()
                    wout()
                    for u in units:
                        u()
                barrier3()
                defer = FRONT_OVL_MAIN and (it + 1 < NT)
                gu2, dn2 = ffn_units(1, 2, defer, "B")
                for u in gu2:
                    u()
                fin_ = with_x(xb, (lambda it_=it: final_norm_store(it_)))
                if defer:
                    pend = [[with_x(xb, u) for u in g_] for g_ in dn2]
                    pend_fin = fin_
                else:
                    for g_ in dn2:
                        for u in g_:
                            with_x(xb, u)()
                    fin_()

        real_S = SB[0]
        saved_rr = rr[0]
        SB[0] = Sched()
        recording[0] = True
        emit_main()
        gseq[:] = rec_seq
        recording[0] = False
        rr[0] = saved_rr
        gi_ctr[0] = 0
        emitted[0] = 0
        mix8[0] = False
        SB[0] = real_S
        emit_main()
        SB[0].add("sp", None, ("outdram",), ())
        S = SB[0]
        S.plan()
        last_out = {}
        for op in S.ops:
            if op["chan"] is not None and op["chan"].startswith("ost"):
                last_out[op["chan"]] = op["token"]
        fin = S.ops[-1]
        for c, (key, val) in last_out.items():
            fin["waits"][key] = max(fin["waits"].get(key, 0), val)
        S.emit(nc, es)
    return nc


def _consts():
    i = np.arange(128)
    ident = np.eye(128, dtype=np.float32).astype(ml_dtypes.bfloat16)
    u32 = (i[:, None] <= i[None, :]).astype(np.float32)
    negmask = np.where(i[None, :] >= i[:, None], 0.0, NEG).astype(np.float32)
    strict = (i[None, :] > i[:, None]).astype(np.float32).astype(ml_dtypes.bfloat16)
    inv = np.zeros((128, 4, 16), np.float32)
    for g, w in enumerate((2, 4, 8, 16)):
        inv[:, g, :] = 1.0 / np.minimum(np.arange(1, 17), w)
    blk = lambda b: (i[:, None] // b) == (i[None, :] // b)
    up = i[None, :] > i[:, None]
    lo = i[None, :] < i[:, None]
    ms = [blk(16), blk(32) & ~blk(16), blk(64) & ~blk(32), ~blk(64), blk(16) & up, blk(16) & lo]
    masks = np.stack([m.astype(np.float32) for m in ms], axis=1).reshape(128, 768).astype(ml_dtypes.bfloat16)
    return ident, u32, negmask, strict, inv.reshape(128, 64), masks


def _col(v, k):
    return np.ascontiguousarray(np.asarray(v, np.float32).reshape(k, 128).T)


def make_in_map(b, x, c, w_ada, b_ada, norm_ffn1, ffn1_gate, ffn1_up, ffn1_down, norm_mix, w_in, conv_w,
                a_log, dt_bias, gdn_norm, pool_w, pool_scale, w_out, norm_ffn2, ffn2_gate, ffn2_up,
                ffn2_down, final_norm):
    ident, u32, negmask, strict, inv, masks = _consts()
    f = lambda a: np.ascontiguousarray(np.asarray(a, np.float32))
    gains = np.concatenate([_col(norm_ffn1[0], KC), _col(norm_mix[0], KC), _col(norm_ffn2[0], KC)], axis=1)
    cw = np.asarray(conv_w[0], np.float32)
    convw_col = np.ascontiguousarray(cw.T.reshape(12, 128, 4).transpose(1, 0, 2).reshape(128, 48))
    return {
        "x": f(x[b]), "c_col": _col(c[b], KC), "w_ada": f(w_ada[0]), "b_ada": f(b_ada[0]).reshape(1, -1),
        "gains_col": np.ascontiguousarray(gains), "final_norm": f(final_norm).reshape(1, -1),
        "f1g": f(ffn1_gate[0]), "f1u": f(ffn1_up[0]), "f1d": f(ffn1_down[0]),
        "f2g": f(ffn2_gate[0]), "f2u": f(ffn2_up[0]), "f2d": f(ffn2_down[0]),
        "w_in": f(w_in[0]), "convw_col": convw_col, "a_log": f(a_log[0]).reshape(1, -1),
        "dt_bias": f(dt_bias[0]).reshape(1, -1), "gdn_norm": f(gdn_norm[0]).reshape(1, -1),
        "pool_w": f(pool_w[0]), "pscale_col": _col(pool_scale[0], 4), "w_out": f(w_out[0]),
        "ident": ident, "u32": u32, "negmask": negmask, "strict01": strict, "invcnt0": inv, "masks": masks,
    }


def kernel(**inputs):
    x = np.asarray(inputs["x"])
    B, T, _ = x.shape
    DFF = np.asarray(inputs["ffn1_gate"]).shape[-1]
    nc = build_program(T=T, TT=512, DFF=DFF)
    in_maps = [make_in_map(b, **inputs) for b in range(B)]
    res = run_bass_kernel_spmd(nc, in_maps, core_ids=list(range(B)))
    return np.stack([np.asarray(r["out"], np.float32) for r in res.results], axis=0)
```

```python
import os
import numpy as np
import ml_dtypes
from contextlib import ExitStack
import concourse.bass as bass
import concourse.mybir as mybir
from concourse.bass_utils import run_bass_kernel_spmd

F32 = mybir.dt.float32
BF16 = mybir.dt.bfloat16
AF = mybir.ActivationFunctionType
ALU = mybir.AluOpType

D = 1024
KC = 8
H = 4
DH = 128
D_IN = 2568
EPS = 1e-6
NEG = -30000.0
NSLOT = 3
ENGS = ("pe", "act", "dve", "pool", "sp")


class Sched:
    def __init__(self):
        self.ops = []
        self.last_w = {}
        self.readers = {}
        self.eng_ops = {e: [] for e in ENGS}
        self.extra = ("ARENA",)

    def add(self, eng, fn, reads=(), writes=(), chan=None):
        idx = len(self.ops)
        if "ARENA" not in writes and "ARENA2" not in writes:
            reads = tuple(reads) + self.extra
        deps = set()
        for r in reads:
            w = self.last_w.get(r)
            if w is not None:
                deps.add(w)
        for w_ in writes:
            w = self.last_w.get(w_)
            if w is not None:
                deps.add(w)
            deps.update(self.readers.get(w_, ()))
        self.ops.append(dict(eng=eng, fn=fn, deps=sorted(deps, reverse=True), chan=chan,
                             ord=len(self.eng_ops[eng])))
        self.eng_ops[eng].append(idx)
        for r in reads:
            self.readers.setdefault(r, []).append(idx)
        for w_ in writes:
            self.last_w[w_] = idx
            self.readers[w_] = []
        return idx

    def plan(self):
        know = {e: {} for e in ENGS}
        chan_cnt = {}
        self.milestones = {e: set() for e in ENGS}
        for op in self.ops:
            e = op["eng"]
            waits = {}
            kn = know[e]
            for d in op["deps"]:
                P = self.ops[d]
                key, val = P["token"]
                if P["chan"] is None and P["eng"] == "pe" and e == "pe":
                    continue
                if kn.get(key, 0) >= val:
                    continue
                waits[key] = max(waits.get(key, 0), val)
                for k2, v2 in P["snap"].items():
                    if kn.get(k2, 0) < v2:
                        kn[k2] = v2
                kn[key] = max(kn.get(key, 0), val)
            op["waits"] = waits
            for (kind, name), val in waits.items():
                if kind == "E":
                    self.milestones[name].add(val)
            if op["chan"] is not None:
                c = op["chan"]
                if c == "init":
                    op["token"] = (("C", c), 1)
                else:
                    chan_cnt[c] = chan_cnt.get(c, 0) + 1
                    op["token"] = (("C", c), chan_cnt[c])
                op["snap"] = dict(kn)
            else:
                op["token"] = (("E", e), op["ord"] + 1)
                snap = dict(kn)
                snap[("E", e)] = op["ord"] + 1
                op["snap"] = snap
        self.chan_cnt = chan_cnt

    def emit(self, nc, es):
        chans = set(op["chan"] for op in self.ops if op["chan"] is not None)
        sem_e = {e: es.enter_context(nc.semaphore("e_" + e)) for e in ENGS}
        sem_c = {c: es.enter_context(nc.semaphore("c_" + c)) for c in sorted(chans)}
        n_init = sum(1 for op in self.ops if op["chan"] == "init")
        ms_count = {}
        for e in ENGS:
            for i, v in enumerate(sorted(self.milestones[e])):
                ms_count[(e, v)] = i + 1
        block = es.enter_context(nc.Block())

        def run(ename, eng):
            for idx in self.eng_ops[ename]:
                op = self.ops[idx]
                for (kind, name), val in op["waits"].items():
                    if kind == "E":
                        eng.wait_ge(sem_e[name], ms_count[(name, val)])
                    elif name == "init":
                        eng.wait_ge(sem_c[name], 16 * n_init)
                    else:
                        eng.wait_ge(sem_c[name], 16 * val)
                if op["fn"] is None:
                    continue
                r = op["fn"](eng)
                last = r[-1] if isinstance(r, (list, tuple)) else r
                if op["chan"] is not None:
                    last.then_inc(sem_c[op["chan"]], 16)
                elif (op["ord"] + 1) in self.milestones[ename]:
                    last.then_inc(sem_e[ename], 1)

        @block.sync
        def _(e):
            run("sp", e)

        @block.tensor
        def _(e):
            run("pe", e)

        @block.scalar
        def _(e):
            run("act", e)

        @block.vector
        def _(e):
            run("dve", e)

        @block.gpsimd
        def _(e):
            run("pool", e)


def build_program(T=8192, TT=512, DFF=2816):
    NST = TT // 128
    NT = T // TT
    NFC = DFF // 128
    NGU = DFF // 256
    NWDB = (NFC + 7) // 8
    assert T % TT == 0 and DFF % 256 == 0 and TT % 128 == 0
    nc = bass.Bass("TRN2", target_bir_lowering=False)
    SB = [Sched()]

    def din(name, shape, dt=F32):
        return nc.dram_tensor(name, list(shape), dt, kind="ExternalInput").ap()

    x_d = din("x", [T, D])
    out_d = nc.dram_tensor("out", [T, D], F32, kind="ExternalOutput").ap()
    ccol_d = din("c_col", [128, KC])
    wada_d = din("w_ada", [D, 9 * D])
    bada_d = din("b_ada", [1, 9 * D])
    gains_d = din("gains_col", [128, 3 * KC])
    final_d = din("final_norm", [1, D])
    fg_d = [din("f1g", [D, DFF]), din("f2g", [D, DFF])]
    fu_d = [din("f1u", [D, DFF]), din("f2u", [D, DFF])]
    fd_d = [din("f1d", [DFF, D]), din("f2d", [DFF, D])]
    win_d = din("w_in", [D, D_IN])
    convw_d = din("convw_col", [128, 12 * 4])
    alog_d = din("a_log", [1, H])
    dtb_d = din("dt_bias", [1, H])
    gn_d = din("gdn_norm", [1, DH])
    poolw_d = din("pool_w", [4, 128, 128])
    pscale_d = din("pscale_col", [128, 4])
    wout_d = din("w_out", [D, D])
    ident_d = din("ident", [128, 128], BF16)
    u32_d = din("u32", [128, 128])
    negmask_d = din("negmask", [128, 128])
    strict_d = din("strict01", [128, 128], BF16)
    invcnt_d = din("invcnt0", [128, 4 * 16])
    masks_d = din("masks", [128, 6 * 128], BF16)

    blocks = []

    def addblk(halves):
        blocks.append(halves)
        return len(blocks) - 1

    GU = [[None] * NGU for _ in range(2)]
    WD = [[[None] * NWDB for _ in range(2)] for _ in range(2)]
    for f in range(2):
        gsrc = fg_d[f].rearrange("(k p) c -> p k c", p=128)
        usrc = fu_d[f].rearrange("(k p) c -> p k c", p=128)
        dsrc = fd_d[f].rearrange("(f p) c -> p f c", p=128)
        for g in range(NGU):
            GU[f][g] = addblk([(gsrc[:, :, g * 256:(g + 1) * 256], (KC, 256)),
                               (usrc[:, :, g * 256:(g + 1) * 256], (KC, 256))])
        for hf in range(2):
            for b in range(NWDB):
                hs = []
                for q in range(2):
                    f0 = b * 8 + q * 4
                    nf = min(4, NFC - f0)
                    if nf > 0:
                        hs.append((dsrc[:, f0:f0 + nf, hf * 512:(hf + 1) * 512], (nf, 512)))
                WD[f][hf][b] = addblk(hs)
    wsrc = win_d.rearrange("(k p) c -> p k c", p=128)
    WIN = {}
    for name, c0 in (("Q", 0), ("K", 512), ("V", 1024), ("Z", 1536), ("P", 2056)):
        WIN[name] = addblk([(wsrc[:, 0:4, c0:c0 + 512], (4, 512)), (wsrc[:, 4:8, c0:c0 + 512], (4, 512))])
    WIN["AB"] = addblk([(wsrc[:, :, 2048:2056], (KC, 8))])
    osrc = wout_d.rearrange("(r p) c -> p r c", p=128)
    WO = [addblk([(osrc[:, 4 * b + 2 * q:4 * b + 2 * q + 2, :], (2, 1024)) for q in range(2)]) for b in range(2)]
    NBLK = len(blocks)
    scr = nc.dram_tensor("wscr", [NBLK, 128, 4096], BF16, kind="Internal").ap()

    def tile_seq():
        s = []
        for f in (0, 1):
            ff = [GU[f][g] for g in range(NGU)]
            for hf in range(2):
                ff += [WD[f][hf][b] for b in range(NWDB)]
            if f == 0:
                s += ff
                s += [WIN["Q"], WIN["K"], WIN["V"], WIN["AB"], WIN["P"], WIN["Z"], WO[0], WO[1]]
            else:
                s += ff
        return s

    seq1 = tile_seq()
    gseq = seq1 * NT

    es = ExitStack()
    with es:
        def sb(name, shape, dt=F32):
            return es.enter_context(nc.sbuf_tensor(name, list(shape), dt))

        Xb = [sb("X0", [128, NST, D]), sb("X1", [128, NST, D])]
        cx = [0]
        xhat = sb("xhat", [128, NST, D], BF16)
        junk2 = sb("junk2", [128, H, 128], BF16)
        hT = sb("hT", [128, KC, TT], BF16)
        ring = [sb("ring%d" % i, [128, 4096], BF16) for i in range(NSLOT)]
        dtmp = [sb("dtmp0", [128, 512])] * 2
        gate_bc = [sb("gate%d" % i, [128, D]) for i in range(3)]
        final_bc = sb("final_bc", [128, D])
        ident = sb("ident_s", [128, 128], BF16)
        u32 = sb("u32_s", [128, 128])
        ones32 = sb("ones32", [128, 128])
        negones32 = sb("negones32", [128, 128])
        negmask = sb("negmask_s", [128, 128])
        strict01 = sb("strict_s", [128, 128], BF16)
        invcnt0 = sb("invcnt_s", [128, 4, 16])
        masks = sb("masks_s", [128, 6, 128], BF16)
        ccol = sb("ccol", [128, KC])
        scol = sb("scol", [128, KC])
        gains = sb("gains", [128, 3, KC])
        modcol = sb("modcol", [128, 9 * KC])
        g1 = sb("g1", [128, 3, KC])
        convw = sb("convw", [128, 12, 4])
        alog_bc = sb("alog_bc", [128, H])
        negA = sb("negA", [128, H])
        dtb_bc = sb("dtb_bc", [128, H])
        gn_bc = sb("gn_bc", [128, DH])
        poolw16 = sb("poolw16", [128, 4, 128], BF16)
        pscale = sb("pscale", [128, 4])
        ss = sb("ss", [128, NST])
        rstd = sb("rstd", [128, NST])
        hist = sb("hist", [128, 12, 3])
        phist = sb("phist", [128, 4, 16])
        S32 = [sb("S32_%d" % i, [128, H, 128]) for i in range(2)]
        S16 = sb("S16", [128, H, 128], BF16)
        abraw = sb("abraw", [128, NST, 8])
        beta = sb("beta", [128, NST, H])
        gt = sb("gt", [128, NST, H])
        gtmp = [sb("gtmp%d" % i, [128, NST, H]) for i in range(3)]
        gcs = sb("gcs", [128, NST, 8])
        egc = sb("egc", [128, NST, H])
        negegc = sb("negegc", [128, NST, H])
        edl = sb("edl", [128, NST, H])
        cdec = sb("cdec", [128, NST, H])
        oss = sb("oss", [128, H])
        orstd = sb("orstd", [128, H])
        one11 = sb("one11", [1, 1])
        ARENA_F32 = 26240
        arena = sb("arena", [128, ARENA_F32])
        apos = [0]

        def carve(nelem, dt=F32, shape=None):
            n32 = nelem if dt == F32 else (nelem + 1) // 2
            a = apos[0]
            apos[0] += n32
            assert apos[0] <= ARENA_F32, "arena overflow"
            v = arena[:, a:a + n32]
            if dt != F32:
                v = v.bitcast(dt)
            if shape is not None:
                names = " ".join("d%d" % i for i in range(len(shape)))
                kw = {"d%d" % i: shape[i] for i in range(1, len(shape))}
                v = v.rearrange("p (%s) -> p %s" % (names, names), **kw)
            return v

        apos[0] = 0
        stage = [carve(2048) for _ in range(4)]
        rowb = [carve(256)[0:1, :] for _ in range(2)]
        badab = [carve(256)[0:1, :] for _ in range(2)]
        poolw32 = carve(512, F32, (4, 128))
        apos[0] = 0
        sg = [carve(TT) for _ in range(2)]
        actT = carve(NFC * TT, BF16, (NFC, TT))
        ffn_end = apos[0]
        apos[0] = 0
        NCS = 3
        pc = [carve(3 + TT) for _ in range(NCS)]
        cacc = [carve(TT) for _ in range(NCS)]
        csq = [carve(TT) for _ in range(NCS)]
        pw = [carve(16 + TT) for _ in range(2)]
        wbuf = [carve(16 + TT) for _ in range(2)]
        ptmp16 = carve(16)
        pooledT = [carve(TT, BF16) for _ in range(2)]
        apos[0] = max(apos[0], ffn_end)
        QT = carve(H * TT, BF16, (H, TT))
        KT = carve(H * TT, BF16, (H, TT))
        VT = carve(H * TT, BF16, (H, TT))
        zsil = [carve(512) for _ in range(NST)]
        otok = [carve(512) for _ in range(2)]
        ontmp = carve(512)
        og = [carve(512, BF16) for _ in range(2)]
        catT = carve(KC * TT, BF16, (KC, TT))
        gU = carve(H * 128, F32, (H, 128))
        ebufs = [carve(H * 128, F32, (H, 128)) for _ in range(2)]
        NCH = 2
        Xp = [[carve(H * 128, BF16, (H, 128)) for _ in range(2)] for _ in range(NCH)]
        Np = [[carve(H * 128, BF16, (H, 128)) for _ in range(2)] for _ in range(NCH)]
        Xf = [carve(H * 128, BF16, (H, 128)) for _ in range(NCH)]
        Nf = [carve(H * 128, BF16, (H, 128)) for _ in range(NCH)]
        RT = [carve(H * 128, BF16, (H, 128)) for _ in range(NST)]
        intraT = [carve(H * 128, BF16, (H, 128)) for _ in range(NST)]
        Vtok = [carve(H * 128, BF16, (H, 128)) for _ in range(NST)]
        kd = [carve(H * 128, BF16, (H, 128)) for _ in range(NST)]
        dbuf = [carve(H * 128, BF16, (H, 128))] * 2
        vnew = [carve(H * 128, BF16, (H, 128))] * 2
        qs = [carve(H * 128, F32, (H, 128))] * 2
        print("arena floats used", apos[0], "of", ARENA_F32, "sbuf remaining", nc.sbuf_bytes_remaining)

        ps = [es.enter_context(nc.psum_tensor("ps%d" % i, [128, 512], F32)) for i in range(8)]
        rr = [0]

        mix8 = [False]

        def nb():
            b = (rr[0] % 8) if mix8[0] else 4 + (rr[0] % 4)
            rr[0] += 1
            return b

        def pipeline_rounds(chains, skew):
            total = max([i * skew + len(ch) for i, ch in enumerate(chains)] + [0])
            rounds = []
            for tau in range(total):
                fs = []
                for i, ch in enumerate(chains):
                    k = tau - i * skew
                    if 0 <= k < len(ch):
                        fs.append(ch[k])
                rounds.append(fs)
            return [(lambda fs_=fs_: [f_() for f_ in fs_]) for fs_ in rounds]

        def pipeline(chains, skew, tag='1'):
            for r_ in pipeline_rounds(chains, skew):
                r_()

        def psb(b):
            return ps[b][:].bitcast(BF16)

        def dma(out, in_, reads, writes, chan, q="sp"):
            return SB[0].add(q, lambda e: e.dma_start(out=out, in_=in_), reads, writes, chan=chan)

        def act(out, in_, func, reads, writes, bias=None, scale=None, accum_out=None):
            kw = {}
            if bias is not None:
                kw["bias"] = bias
            if scale is not None:
                kw["scale"] = scale
            if accum_out is not None:
                kw["accum_out"] = accum_out
            return SB[0].add("act", lambda e: e.activation(out=out, in_=in_, func=func, **kw), reads, writes)

        def tt(eng, out, in0, in1, op, reads, writes):
            return SB[0].add(eng, lambda e: e.tensor_tensor(out=out, in0=in0, in1=in1, op=op), reads, writes)

        def ts(eng, out, in0, s1, s2, op0, op1, reads, writes):
            if op1 is None:
                return SB[0].add(eng, lambda e: e.tensor_scalar(out=out, in0=in0, scalar1=s1, scalar2=None, op0=op0),
                             reads, writes)
            return SB[0].add(eng, lambda e: e.tensor_scalar(out=out, in0=in0, scalar1=s1, scalar2=s2, op0=op0, op1=op1),
                         reads, writes)

        def stt(eng, out, in0, scalar, in1, op0, op1, reads, writes):
            return SB[0].add(eng, lambda e: e.scalar_tensor_tensor(out=out, in0=in0, scalar=scalar, in1=in1,
                                                               op0=op0, op1=op1), reads, writes)

        def cp(eng, out, in_, reads, writes):
            if eng == "act":
                return SB[0].add("act", lambda e: e.copy(out=out, in_=in_), reads, writes)
            return SB[0].add(eng, lambda e: e.tensor_copy(out=out, in_=in_), reads, writes)

        def mms(items, reads, writes):
            def fn(e):
                r = []
                for (o, l, rh, st, sp_) in items:
                    r.append(e.matmul(o, lhsT=l, rhs=rh, start=st, stop=sp_))
                return r
            return SB[0].add("pe", fn, reads, writes)

        def tps(items, reads, writes):
            def fn(e):
                return [e.transpose(o, i, ident[:]) for (o, i) in items]
            return SB[0].add("pe", fn, reads, writes)

        def memset(eng, ap, val, writes):
            return SB[0].add(eng, lambda e: e.memset(ap, val), (), writes)

        bscr = sb("bscr", [128, 1])

        def barrier(eng="dve"):
            SB[0].add(eng, lambda e: e.memset(bscr[:], 0.0), (), ("ARENA",))

        AR = ()

        def barrier2():
            SB[0].add("dve", lambda e: e.memset(bscr[:], 0.0), (), ("ARENA2",))

        class tok2:
            def __enter__(self_):
                self_.old = SB[0].extra
                SB[0].extra = ("ARENA", "ARENA2")

            def __exit__(self_, *a):
                SB[0].extra = self_.old

        def in_tok2(fn):
            def g():
                with tok2():
                    fn()
            return g

        def with_x(xb, fn):
            def g():
                old = cx[0]
                cx[0] = xb
                fn()
                cx[0] = old
            return g

        def ld(out, in_, w):
            dma(out, in_, (), (w,), "init")

        ld(ident[:], ident_d, "ident")
        ld(u32[:], u32_d, "u32")
        ld(negmask[:], negmask_d, "negmask")
        ld(strict01[:], strict_d, "strict01")
        ld(invcnt0[:].rearrange("p g c -> p (g c)"), invcnt_d, "invcnt0")
        ld(masks[:].rearrange("p g c -> p (g c)"), masks_d, "masks")
        ld(ccol[:], ccol_d, "ccol")
        ld(gains[:].rearrange("p s k -> p (s k)"), gains_d, "gains")
        ld(final_bc[:], final_d.partition_broadcast(128), "final_bc")
        ld(convw[:].rearrange("p c k -> p (c k)"), convw_d, "convw")
        ld(alog_bc[:], alog_d.partition_broadcast(128), "alog")
        ld(dtb_bc[:], dtb_d.partition_broadcast(128), "dtb")
        ld(gn_bc[:], gn_d.partition_broadcast(128), "gn")
        ld(poolw32[:], poolw_d.rearrange("g i o -> i g o"), "poolw32")
        ld(pscale[:], pscale_d, "pscale")

        memset("dve", ones32[:], 1.0, ("ones32",))
        memset("dve", negones32[:], -1.0, ("negones32",))
        memset("dve", one11[:], 1.0, ("one11",))
        for i in range(2):
            memset("pool", S32[i][:], 0.0, tuple("S32_%d_h%d" % (i, h_) for h_ in range(H)))
        memset("pool", S16[:], 0.0, tuple("S16_h%d" % h_ for h_ in range(H)))
        memset("pool", hist[:], 0.0, ("hist",))
        memset("pool", phist[:], 0.0, ("phist",))
        for i in range(NSLOT):
            memset("pool", ring[i][:], 0.0, ("w%d" % i,))
        cp("dve", poolw16[:], poolw32[:], ("poolw32",), ("poolw16",))
        act(negA[:], alog_bc[:], AF.Exp, ("alog",), ("negA",))
        ts("dve", negA[:], negA[:], -1.0, None, ALU.mult, None, ("negA",), ("negA",))
        act(scol[:], ccol[:], AF.Silu, ("ccol",), ("scol",))

        wada_v = wada_d.rearrange("(k p) c -> p k c", p=128)
        colbank = 7
        NCB = 9 * D // 256
        for cb in range(NCB):
            stg = stage[cb % 4]
            sname = "stage%d" % (cb % 4)
            sv = stg.rearrange("p (k c) -> p k c", c=256)
            dma(sv, wada_v[:, :, cb * 256:(cb + 1) * 256], (), (sname,), "stg%d" % (cb % 4))
            bb = badab[cb % 2]
            dma(bb[:], bada_d[:, cb * 256:(cb + 1) * 256], (), ("badab%d" % (cb % 2),), "bada%d" % (cb % 2))
            bk = 4 + (cb % 2)
            mms([(ps[bk][0:1, 0:256], scol[:, k:k + 1], sv[:, k, :], k == 0, k == KC - 1) for k in range(KC)],
                (sname, "scol"), ("ps%d" % bk,))
            rb = rowb[cb % 2]
            rn = "rowb%d" % (cb % 2)
            tt("dve", rb[:], ps[bk][0:1, 0:256], bb[:], ALU.add, ("ps%d" % bk, "badab%d" % (cb % 2)), (rn,))
            mms([(ps[colbank][:, 2 * cb + j:2 * cb + j + 1], rb[0:1, j * 128:(j + 1) * 128], one11[0:1, 0:1], True, True)
                 for j in range(2)], (rn, "one11"), ("pscol",))
            m = (cb * 256) // D
            if m in (2, 5, 8):
                gi = m // 3
                c0 = (cb * 256) % D
                bk2 = 6
                mms([(ps[bk2][:, 0:256], ones32[0:1, :], rb[0:1, :], True, True)], (rn, "ones32"), ("ps6",))
                act(gate_bc[gi][:, c0:c0 + 256], ps[bk2][:, 0:256], AF.Copy, ("ps6",), ("gate%d" % gi,),
                    scale=(1.0 if gi == 1 else 0.5))
        cp("dve", modcol[:], ps[colbank][:, 0:9 * KC], ("pscol",), ("modcol",))
        for s_ in range(3):
            stt("dve", g1[:, s_, :], modcol[:, (3 * s_ + 1) * KC:(3 * s_ + 2) * KC], 1.0, gains[:, s_, :],
                ALU.add, ALU.mult, ("modcol", "gains"), ("g1",))

        cast_engs = ("act", "dve")
        hcount = [0]
        for blk in range(NBLK):
            slot = blk % NSLOT
            wn = "w%d" % slot
            for q, (src, (a_, b_)) in enumerate(blocks[blk]):
                si = hcount[0] % 4
                hcount[0] += 1
                n = a_ * b_
                sv = stage[si][:, 0:n].rearrange("p (a b) -> p a b", b=b_)
                dma(sv, src, (), ("stage%d" % si,), "stg%d" % si)
                cp(cast_engs[hcount[0] % 2], ring[slot][:, q * 2048:q * 2048 + n], stage[si][:, 0:n],
                   ("stage%d" % si,), (wn,))
            dma(scr[blk], ring[slot][:], (wn,), ("scr%d" % blk,), "wst%d" % slot)
        barrier("dve")

        emitted = [0]
        recording = [False]
        rec_seq = []

        def acquire(i, oldest=None):
            if recording[0]:
                return i % NSLOT
            lim = min((i if oldest is None else oldest) + NSLOT - 1, len(gseq) - 1)
            while emitted[0] <= lim:
                j = emitted[0]
                sl = j % NSLOT
                dma(ring[sl][:], scr[gseq[j]], ("scr%d" % gseq[j],), ("w%d" % sl,), "wld%d" % sl)
                emitted[0] += 1
            return i % NSLOT

        gi_ctr = [0]

        def next_block(expect, oldest=None):
            i = gi_ctr[0]
            if recording[0]:
                rec_seq.append(expect)
            else:
                assert gseq[i] == expect, (i, gseq[i], expect)
            gi_ctr[0] += 1
            return acquire(i, oldest), i

        def xres(st, hf=None):
            if hf is None:
                return ("x%d_%d_0" % (cx[0], st), "x%d_%d_1" % (cx[0], st))
            return ("x%d_%d_%d" % (cx[0], st, hf),)

        def norm_to_hT(s_):
            for st in range(NST):
                act(xhat[:, st, :], Xb[cx[0]][:, st, :], AF.Square, xres(st), ("xhat%d" % st, "ss%d" % st),
                    accum_out=ss[:, st:st + 1])
            ts("dve", rstd[:], ss[:], 1.0 / D, EPS, ALU.mult, ALU.add, tuple("ss%d" % st_ for st_ in range(NST)), ("rstd",))
            act(rstd[:], rstd[:], AF.Sqrt, ("rstd",), ("rstd",))
            SB[0].add("dve", lambda e: e.reciprocal(out=rstd[:], in_=rstd[:]), ("rstd",), ("rstd",))
            for st in range(NST):
                if st % 2 == 0:
                    ts("dve", xhat[:, st, :], Xb[cx[0]][:, st, :], rstd[:, st:st + 1], None, ALU.mult, None,
                       xres(st) + ("rstd",), ("xhat%d" % st,))
                else:
                    act(xhat[:, st, :], Xb[cx[0]][:, st, :], AF.Copy, xres(st) + ("rstd",), ("xhat%d" % st,),
                        scale=rstd[:, st:st + 1])
            for k in range(KC):
                b = nb()
                pv = psb(b)
                tps([(pv[:, st * 128:(st + 1) * 128], xhat[:, st, k * 128:(k + 1) * 128]) for st in range(NST)],
                    tuple("xhat%d" % st for st in range(NST)) + ("ident",), ("ps%d" % b,))
                if k % 2 == 0:
                    act(hT[:, k, :], pv[:, 0:TT], AF.Identity, ("ps%d" % b, "g1", "modcol"), ("hT%d" % k,),
                        bias=modcol[:, 3 * s_ * KC + k:3 * s_ * KC + k + 1], scale=g1[:, s_, k:k + 1])
                else:
                    ts("dve", hT[:, k, :], pv[:, 0:TT], g1[:, s_, k:k + 1],
                       modcol[:, 3 * s_ * KC + k:3 * s_ * KC + k + 1], ALU.mult, ALU.add,
                       ("ps%d" % b, "g1", "modcol"), ("hT%d" % k,))

        HT_ALL = tuple("hT%d" % k for k in range(KC))

        def resid_add(b, st, hf, gi, k):
            d_ = dtmp[k % 2]
            dn = "dtmp0"
            tt("dve", d_[:], ps[b][:], gate_bc[gi][:, hf * 512:(hf + 1) * 512], ALU.mult,
               ("ps%d" % b, "gate%d" % gi), (dn,))
            xs = Xb[cx[0]][:, st, hf * 512:(hf + 1) * 512]
            tt("pool", xs, xs, d_[:], ALU.add, (dn,) + xres(st, hf), xres(st, hf))

        def ffn_units(f, s_, overlapped):
            units = [lambda: norm_to_hT(s_)]
            st_ = {}

            def gu_unit(g, j):
                def u():
                    if j == 0:
                        slot, _ = next_block(GU[f][g])
                        st_["gu"] = slot
                    slot = st_["gu"]
                    W = ring[slot][:].rearrange("p (w k c) -> p w k c", w=2, k=KC)
                    wn = "w%d" % slot
                    fc = 2 * g + j
                    bg = fc % 2
                    bu = 2 + fc % 2
                    mms([(ps[bg][:, 0:TT], W[:, 0, k, j * 128:(j + 1) * 128], hT[:, k, :], k == 0, k == KC - 1)
                         for k in range(KC)], (wn,) + HT_ALL, ("ps%d" % bg,))
                    mms([(ps[bu][:, 0:TT], W[:, 1, k, j * 128:(j + 1) * 128], hT[:, k, :], k == 0, k == KC - 1)
                         for k in range(KC)], (wn,) + HT_ALL, ("ps%d" % bu,))
                    act(sg[fc % 2][:], ps[bg][:, 0:TT], AF.Silu, ("ps%d" % bg,), ("sg%d" % (fc % 2),))
                    tt("dve", actT[:, fc, :], sg[fc % 2][:], ps[bu][:, 0:TT], ALU.mult,
                       ("sg%d" % (fc % 2), "ps%d" % bu), ("actT%d" % fc,))
                return u
            for g in range(NGU):
                for j in range(2):
                    units.append(gu_unit(g, j))

            def wd_unit(hf, b, st):
                base = 0 if (overlapped or hf == 1) else 4

                def u():
                    if st == 0:
                        slot, _ = next_block(WD[f][hf][b])
                        st_["wd"] = slot
                    slot = st_["wd"]
                    W = ring[slot][:].rearrange("p (f c) -> p f c", c=512)
                    wn = "w%d" % slot
                    nf = min(8, NFC - b * 8)
                    mms([(ps[base + st][:], actT[:, b * 8 + q, st * 128:(st + 1) * 128], W[:, q, :],
                          (b == 0 and q == 0), (b == NWDB - 1 and q == nf - 1)) for q in range(nf)],
                        (wn,) + tuple("actT%d" % (b * 8 + q) for q in range(nf)), ("ps%d" % (base + st),))
                return u

            def res_unit(hf, st):
                base = 0 if (overlapped or hf == 1) else 4
                return lambda: resid_add(base + st, st, hf, (0 if s_ == 0 else 2), 0)
            for hf in range(2):
                for b in range(NWDB):
                    for st in range(NST):
                        units.append(wd_unit(hf, b, st))
                for st in range(NST):
                    units.append(res_unit(hf, st))
            return [in_tok2(u) for u in units]

        def mixer(it):
            mix8[0] = True
            norm_to_hT(1)
            hsl = lambda st: slice(st * 128, (st + 1) * 128)
            blkW = {}

            def get_w(name, c=512):
                if name not in blkW:
                    slot, _ = next_block(WIN[name])
                    if name == "AB":
                        blkW[name] = (ring[slot][:, 0:64].rearrange("p (k c) -> p k c", c=8), "w%d" % slot)
                    else:
                        blkW[name] = (ring[slot][:].rearrange("p (k c) -> p k c", c=512), "w%d" % slot)
                return blkW[name]

            def conv_chain(gi_, name, h):
                c = gi_ * 4 + h
                i = c % NCS
                p_, ca, cs = pc[i], cacc[i], csq[i]
                pn, can, csn = "pc%d" % i, "cacc%d" % i, "csq%d" % i

                def s0():
                    W, wn = get_w(name)
                    b = nb()
                    mms([(ps[b][:, 0:TT], W[:, k, h * 128:(h + 1) * 128], hT[:, k, :], k == 0, k == KC - 1)
                         for k in range(KC)], (wn,) + HT_ALL, ("ps%d" % b,))
                    cp("pool", p_[:, 0:3], hist[:, c, :], ("hist%d" % c, "hist"), (pn + "h",))
                    act(p_[:, 3:3 + TT], ps[b][:, 0:TT], AF.Copy, ("ps%d" % b,), (pn,))
                    cp("pool", hist[:, c, :], p_[:, TT:TT + 3], (pn, pn + "h"), ("hist%d" % c,))

                def s1():
                    act(ca[:], p_[:, 0:TT], AF.Copy, (pn, pn + "h", "convw"), (can,), scale=convw[:, c, 0:1])
                    for k in (1, 2, 3):
                        stt("dve", ca[:], p_[:, k:k + TT], convw[:, c, k:k + 1], ca[:], ALU.mult, ALU.add,
                            (pn, pn + "h", "convw", can), (can,))

                def s2():
                    if name == "V":
                        act(VT[:, h, :], ca[:], AF.Silu, (can,), ("VT%d" % h,))
                    else:
                        act(cs[:], ca[:], AF.Silu, (can,), (csn,))
                        tt("pool", ca[:], cs[:], cs[:], ALU.mult, (csn, can), (can,))

                def s3():
                    b2 = nb()
                    mms([(ps[b2][:, 0:TT], ones32[:], ca[:], True, True)], (can, "ones32"), ("ps%d" % b2,))
                    ts("dve", ca[:], ps[b2][:, 0:TT], EPS, None, ALU.add, None, ("ps%d" % b2, can), (can,))

                def s4():
                    dst = QT if name == "Q" else KT
                    act(ca[:], ca[:], AF.Sqrt, (can,), (can,))
                    SB[0].add("dve", lambda e: e.reciprocal(out=ca[:], in_=ca[:]), (can,), (can,))
                    stt("dve", dst[:, h, :], cs[:], (DH ** -0.5 if name == "Q" else 1.0), ca[:],
                        ALU.mult, ALU.mult, (csn, can), ("%sT%d" % (name, h),))
                return [s0, s1, s2] if name == "V" else [s0, s1, s2, s3, s4]

            t0, t1, t2 = gtmp

            def g0():
                W, wn = get_w("AB")
                for st in range(NST):
                    b = nb()
                    mms([(ps[b][:, 0:8], hT[:, k, hsl(st)], W[:, k, :], k == 0, k == KC - 1) for k in range(KC)],
                        (wn,) + HT_ALL, ("ps%d" % b,))
                    cp("dve", abraw[:, st, :], ps[b][:, 0:8], ("ps%d" % b,), ("abraw",))

            def g1():
                act(beta[:], abraw[:, :, 0:4], AF.Sigmoid, ("abraw",), ("beta",))
                tt("dve", t0[:], abraw[:, :, 4:8], dtb_bc[:].unsqueeze(1).to_broadcast([128, NST, H]), ALU.add,
                   ("abraw", "dtb"), ("gtmp0",))
                ts("dve", t1[:], t0[:], -1.0, None, ALU.mult, None, ("gtmp0",), ("gtmp1",))
                tt("dve", t1[:], t1[:], t0[:], ALU.max, ("gtmp0", "gtmp1"), ("gtmp1",))

            def g2():
                act(t1[:], t1[:], AF.Exp, ("gtmp1",), ("gtmp1",), scale=-1.0)
                ts("dve", t1[:], t1[:], 1.0, None, ALU.add, None, ("gtmp1",), ("gtmp1",))
                ts("dve", t2[:], t0[:], 0.0, None, ALU.max, None, ("gtmp0",), ("gtmp2",))

            def g3():
                act(t1[:], t1[:], AF.Ln, ("gtmp1",), ("gtmp1",))
                tt("dve", t2[:], t2[:], t1[:], ALU.add, ("gtmp1", "gtmp2"), ("gtmp2",))
                tt("dve", gt[:], t2[:], negA[:].unsqueeze(1).to_broadcast([128, NST, H]), ALU.mult,
                   ("gtmp2", "negA"), ("gt",))

            def g4():
                for st in range(NST):
                    b = nb()
                    mms([(ps[b][:, 0:4], u32[:], gt[:, st, :], True, True),
                         (ps[b][:, 4:8], ones32[:], gt[:, st, :], True, True)], ("gt", "u32", "ones32"),
                        ("ps%d" % b,))
                    cp("dve", gcs[:, st, :], ps[b][:, 0:8], ("ps%d" % b,), ("gcs",))

            def g5():
                act(egc[:], gcs[:, :, 0:4], AF.Exp, ("gcs",), ("egc",))
                tt("dve", edl[:], gcs[:, :, 4:8], gcs[:, :, 0:4], ALU.subtract, ("gcs",), ("edl",))
                act(cdec[:], gcs[:, :, 4:8], AF.Exp, ("gcs",), ("cdec",))
                ts("dve", negegc[:], egc[:], -1.0, None, ALU.mult, None, ("egc",), ("negegc",))
                act(edl[:], edl[:], AF.Exp, ("edl",), ("edl",))

            def pool_chain(g):
                win = 2 ** (g + 1)
                p_ = pw[g % 2]
                pn = "pw%d" % (g % 2)
                po = pooledT[g % 2]
                pon = "pooledT%d" % (g % 2)

                def p0():
                    W, wn = get_w("P")
                    b = nb()
                    mms([(ps[b][:, 0:TT], W[:, k, g * 128:(g + 1) * 128], hT[:, k, :], k == 0, k == KC - 1)
                         for k in range(KC)], (wn,) + HT_ALL, ("ps%d" % b,))
                    cp("pool", p_[:, 0:16], phist[:, g, :], ("phist%d" % g, "phist"), (pn + "h",))
                    act(p_[:, 16:16 + TT], ps[b][:, 0:TT], AF.Copy, ("ps%d" % b,), (pn,))
                    cp("pool", phist[:, g, :], p_[:, TT:TT + 16], (pn, pn + "h"), ("phist%d" % g,))

                def p1():
                    src = p_
                    srcn = (pn, pn + "h")
                    for k in range(g + 1):
                        sh = 2 ** k
                        lo = 2 ** (k + 1)
                        dstb = wbuf[k % 2]
                        dn = "wbuf%d" % (k % 2)
                        tt("pool", dstb[:, lo:16 + TT], src[:, lo:16 + TT],
                           src[:, lo - sh:16 + TT - sh], ALU.add, srcn, (dn,))
                        src = dstb
                        srcn = (dn,)
                    stt("dve", po[:], src[:, 16:16 + TT], 1.0 / win, p_[:, 16:16 + TT], ALU.mult, ALU.subtract,
                        srcn + (pn,), (pon,))
                    if it == 0:
                        tt("dve", ptmp16[:], src[:, 16:32], invcnt0[:, g, :], ALU.mult, srcn + ("invcnt0",),
                           ("ptmp16",))
                        tt("dve", po[:, 0:16], ptmp16[:], p_[:, 16:32], ALU.subtract, ("ptmp16", pn, pon), (pon,))

                def p2():
                    b2 = nb()
                    mms([(ps[b2][:, 0:TT], poolw16[:, g, :], po[:], True, True)], (pon, "poolw16"), ("ps%d" % b2,))
                    act(catT[:, 4 + g, :], ps[b2][:, 0:TT], AF.Copy, ("ps%d" % b2, "pscale"),
                        ("catT%d" % (4 + g),), scale=pscale[:, g:g + 1])
                return [p0, p1, p2]

            chains = [conv_chain(gi_, name, h) for gi_, name in enumerate(("Q", "K", "V")) for h in range(H)]
            chains.append([g0, g1, g2, g3, g4, g5])
            chains += [pool_chain(g) for g in range(4)]

            def z_chain(st):
                def z0():
                    W, wn = get_w("Z")
                    b = nb()
                    mms([(ps[b][:], hT[:, k, hsl(st)], W[:, k, :], k == 0, k == KC - 1) for k in range(KC)],
                        (wn,) + HT_ALL, ("ps%d" % b,))
                    act(zsil[st][:], ps[b][:], AF.Silu, ("ps%d" % b,), ("zsil%d" % st,))
                return [z0]
            chains += [z_chain(st) for st in range(NST)]
            with tok2():
                pipeline(chains, 2)
            mix8[0] = False
            QKV = tuple("%sT%d" % (n_, h) for n_ in "QKV" for h in range(H))

            def intra_steps(st, ch):
                ebuf = ebufs[st % 2]
                en = "ebuf%d" % (st % 2)
                X0 = Xf[ch]
                xn = "Xf%d" % ch

                def i0_():
                    tt("dve", gU[:], u32[:].unsqueeze(1).to_broadcast([128, H, 128]),
                       gt[:, st, :].unsqueeze(2).to_broadcast([128, H, 128]), ALU.mult, ("u32", "gt"), ("gU",))
                    bD = nb()
                    items = []
                    for h in range(H):
                        items.append((ps[bD][:, hsl(h)], ones32[:], gU[:, h, :], True, False))
                        items.append((ps[bD][:, hsl(h)], gU[:, h, :], negones32[:], False, True))
                    mms(items, ("gU", "ones32", "negones32"), ("ps%d" % bD,))
                    tt("dve", ebuf[:], ps[bD][:].rearrange("p (h c) -> p h c", h=H),
                       negmask[:].unsqueeze(1).to_broadcast([128, H, 128]), ALU.add,
                       ("ps%d" % bD, "negmask"), (en,))

                def i1_():
                    act(ebuf[:], ebuf[:], AF.Exp, (en,), (en,))

                def i2_():
                    bK = nb()
                    mms([(ps[bK][:, hsl(h)], KT[:, h, hsl(st)], KT[:, h, hsl(st)], True, True) for h in range(H)],
                        QKV, ("ps%d" % bK,))
                    bQ = nb()
                    mms([(ps[bQ][:, hsl(h)], KT[:, h, hsl(st)], QT[:, h, hsl(st)], True, True) for h in range(H)],
                        QKV, ("ps%d" % bQ,))
                    for h in range(H):
                        stt("dve", X0[:, h, :], ps[bK][:, hsl(h)], beta[:, st, h:h + 1], ebuf[:, h, :],
                            ALU.mult, ALU.mult, ("ps%d" % bK, "beta", en), (xn + "_h%d" % h,))
                    tt("dve", intraT[st][:], ps[bQ][:].rearrange("p (h c) -> p h c", h=H), ebuf[:], ALU.mult,
                       ("ps%d" % bQ, en), ("intraT%d" % st,))

                def i3_():
                    bT = nb()
                    pv = psb(bT)
                    tps([(pv[:, hsl(h)], KT[:, h, hsl(st)]) for h in range(H)], QKV + ("ident",), ("ps%d" % bT,))
                    bV = nb()
                    pvv = psb(bV)
                    tps([(pvv[:, hsl(h)], VT[:, h, hsl(st)]) for h in range(H)], QKV + ("ident",), ("ps%d" % bV,))
                    for h in range(H):
                        act(kd[st][:, h, :], pv[:, hsl(h)], AF.Copy, ("ps%d" % bT, "edl"), ("kd%d_h%d" % (st, h),),
                            scale=edl[:, st, h:h + 1])
                    cp("dve", Vtok[st][:], pvv[:, 0:512].rearrange("p (h c) -> p h c", h=H), ("ps%d" % bV,),
                       ("Vtok%d" % st,))

                def i4_():
                    bN = nb()
                    pv2 = psb(bN)
                    tps([(pv2[:, hsl(h)], X0[:, h, :]) for h in range(H)],
                        tuple(xn + "_h%d" % h_ for h_ in range(H)) + ("ident",), ("ps%d" % bN,))
                    cp("act", Nf[ch][:], pv2[:, 0:512].rearrange("p (h c) -> p h c", h=H), ("ps%d" % bN,),
                       ("Nf%d" % ch,))
                return [i0_, i1_, i2_, i3_, i4_]

            def mask_bc(mi):
                return masks[:, mi, :].unsqueeze(1).to_broadcast([128, H, 128])

            def hview(b):
                return ps[b][:].rearrange("p (h c) -> p h c", h=H)

            def inv_steps(st, ch):
                A1, A2 = Xp[ch]
                B1, B2 = Np[ch]
                a1, a2 = "Xp%d_0" % ch, "Xp%d_1" % ch
                b1, b2 = "Np%d_0" % ch, "Np%d_1" % ch
                xf, nf, rt = tuple("Xf%d_h%d" % (ch, h_) for h_ in range(H)), "Nf%d" % ch, "RT%d" % st
                R = RT[st]
                steps = []

                def base0():
                    tt("pool", A1[:], Xf[ch][:], mask_bc(4), ALU.mult, xf + ("masks",), (a1,))
                    tt("pool", B1[:], Nf[ch][:], mask_bc(5), ALU.mult, (nf, "masks"), (b1,))
                    tt("dve", R[:], ident[:].unsqueeze(1).to_broadcast([128, H, 128]), A1[:], ALU.subtract,
                       (a1, "ident"), (rt,))
                steps.append(base0)

                def level(l):
                    def f():
                        if l % 2 == 1:
                            Xa, Na, Xb, Nb_, xa, na, xb, nb_ = A1, B1, A2, B2, a1, b1, a2, b2
                        else:
                            Xa, Na, Xb, Nb_, xa, na, xb, nb_ = A2, B2, A1, B1, a2, b2, a1, b1
                        p1 = nb()
                        mms([(ps[p1][:, hsl(h)], Xa[:, h, :], Na[:, h, :], True, True) for h in range(H)], (xa, na),
                            ("ps%d" % p1,))
                        cp("act", Nb_[:], hview(p1), ("ps%d" % p1,), (nb_,))
                        if l < 3:
                            p2 = nb()
                            mms([(ps[p2][:, hsl(h)], Na[:, h, :], Xa[:, h, :], True, True) for h in range(H)],
                                (xa, na), ("ps%d" % p2,))
                            cp("dve", Xb[:], hview(p2), ("ps%d" % p2,), (xb,))
                        p3 = nb()
                        mms([(ps[p3][:, hsl(h)], Nb_[:, h, :], R[:, h, :], True, True) for h in range(H)],
                            (nb_, rt), ("ps%d" % p3,))
                        tt("dve", R[:], hview(p3), R[:], ALU.add, ("ps%d" % p3, rt), (rt,))
                    return f
                for l in (1, 2, 3):
                    steps.append(level(l))

                def wtrans():
                    bT = nb()
                    pv = psb(bT)
                    tps([(pv[:, hsl(h)], R[:, h, :]) for h in range(H)], (rt, "ident"), ("ps%d" % bT,))
                    cp("act", A1[:], pv[:, 0:512].rearrange("p (h c) -> p h c", h=H), ("ps%d" % bT,), (a1,))
                steps.append(wtrans)

                def merge(mi, last):
                    def f1():
                        tt("pool", B1[:], Xf[ch][:], mask_bc(mi), ALU.mult, xf + ("masks",), (b1,))
                        tt("pool", B2[:], Nf[ch][:], mask_bc(mi), ALU.mult, (nf, "masks"), (b2,))
                        p1 = nb()
                        mms([(ps[p1][:, hsl(h)], B2[:, h, :], R[:, h, :], True, True) for h in range(H)], (b2, rt),
                            ("ps%d" % p1,))
                        cp("act", A2[:], hview(p1), ("ps%d" % p1,), (a2,))
                        if not last:
                            p2 = nb()
                            mms([(ps[p2][:, hsl(h)], B1[:, h, :], A1[:, h, :], True, True) for h in range(H)],
                                (b1, a1), ("ps%d" % p2,))
                            cp("dve", B1[:], hview(p2), ("ps%d" % p2,), (b1,))

                    def f2():
                        p3 = nb()
                        mms([(ps[p3][:, hsl(h)], A1[:, h, :], A2[:, h, :], True, True) for h in range(H)], (a1, a2),
                            ("ps%d" % p3,))
                        if not last:
                            p4 = nb()
                            mms([(ps[p4][:, hsl(h)], R[:, h, :], B1[:, h, :], True, True) for h in range(H)],
                                (rt, b1), ("ps%d" % p4,))
                        tt("dve", R[:], R[:], hview(p3), ALU.subtract, ("ps%d" % p3, rt), (rt,))
                        if not last:
                            tt("dve", A1[:], A1[:], hview(p4), ALU.subtract, ("ps%d" % p4, a1), (a1,))
                    return [f1, f2]
                steps += merge(1, False) + merge(2, False) + merge(3, True)
                return steps

            back = []
            for st0 in range(0, NST, NCH):
                sts = list(range(st0, min(st0 + NCH, NST)))
                for st in sts:
                    for f_ in intra_steps(st, st % NCH):
                        back.append((f_, float(os.environ.get("K_INTRA_W", "0.0"))))
                for r_ in pipeline_rounds([inv_steps(st, st % NCH) for st in sts], 1):
                    back.append((r_, 1.0))
            S16N = tuple("S16_h%d" % h_ for h_ in range(H))
            VNN = tuple("vnew0_h%d" % h_ for h_ in range(H))
            KDN = lambda st_: tuple("kd%d_h%d" % (st_, h_) for h_ in range(H))

            def rec_A(st):
                cur = (it * NST + st) % 2
                Sc, Sn = S32[cur], S32[1 - cur]
                scn, snn = "S32_%d" % cur, "S32_%d" % (1 - cur)
                d_, dn = dbuf[0], "dbuf0"
                q_, qn = qs[0], "qs0"
                v_, vn = vnew[0], "vnew0"

                def a1():
                    bA = nb()
                    mms([(ps[bA][:, hsl(h)], KT[:, h, hsl(st)], S16[:, h, :], True, True) for h in range(H)],
                        QKV + S16N, ("ps%d" % bA,))
                    bB = nb()
                    mms([(ps[bB][:, hsl(h)], QT[:, h, hsl(st)], S16[:, h, :], True, True) for h in range(H)],
                        QKV + S16N, ("ps%d" % bB,))
                    for h in range(H):
                        stt("dve", d_[:, h, :], ps[bA][:, hsl(h)], negegc[:, st, h:h + 1], Vtok[st][:, h, :],
                            ALU.mult, ALU.add, ("ps%d" % bA, "negegc", "Vtok%d" % st), (dn + "_h%d" % h,))
                        act(q_[:, h, :], ps[bB][:, hsl(h)], AF.Copy, ("ps%d" % bB, "egc"), (qn + "_h%d" % h,),
                            scale=egc[:, st, h:h + 1])

                def a2():
                    bC = nb()
                    mms([(ps[bC][:, hsl(h)], RT[st][:, h, :], d_[:, h, :], True, True) for h in range(H)],
                        ("RT%d" % st,) + tuple(dn + "_h%d" % h_ for h_ in range(H)), ("ps%d" % bC,))
                    for h in range(H):
                        ts("dve", v_[:, h, :], ps[bC][:, hsl(h)], beta[:, st, h:h + 1], None, ALU.mult, None,
                           ("ps%d" % bC, "beta"), (vn + "_h%d" % h,))

                def a3():
                    bE = nb()
                    mms([(ps[bE][:, hsl(h)], kd[st][:, h, :], v_[:, h, :], True, True) for h in range(H)],
                        KDN(st) + VNN, ("ps%d" % bE,))
                    bD = nb()
                    mms([(ps[bD][:, hsl(h)], intraT[st][:, h, :], v_[:, h, :], True, True) for h in range(H)],
                        ("intraT%d" % st,) + VNN, ("ps%d" % bD,))
                    for h in range(H):
                        stt("dve", S16[:, h, :], Sc[:, h, :], cdec[:, st, h:h + 1], ps[bE][:, hsl(h)],
                            ALU.mult, ALU.add, (scn + "_h%d" % h, "cdec", "ps%d" % bE), ("S16_h%d" % h,))
                    for h in range(H):
                        stt("dve", Sn[:, h, :], Sc[:, h, :], cdec[:, st, h:h + 1], ps[bE][:, hsl(h)],
                            ALU.mult, ALU.add, (scn + "_h%d" % h, "cdec", "ps%d" % bE), (snn + "_h%d" % h,))
                    o_ = otok[st % 2]
                    on_ = "otok%d" % (st % 2)
                    tt("dve", o_[:], ps[bD][:], q_[:].rearrange("p h c -> p (h c)"), ALU.add,
                       ("ps%d" % bD,) + tuple(qn + "_h%d" % h_ for h_ in range(H)), (on_,))
                return [a1, a2, a3]

            def rec_B(st):
                o_ = otok[st % 2]
                on_ = "otok%d" % (st % 2)
                z_ = zsil[st]
                zn = "zsil%d" % st
                g_ = og[st % 2]
                gn_ = "og%d" % (st % 2)

                def b1():
                    for h in range(H):
                        act(junk2[:, h, :], o_[:, hsl(h)], AF.Square, (on_,), ("junk2_h%d" % h, "oss_h%d" % h),
                            accum_out=oss[:, h:h + 1])
                    ts("dve", orstd[:], oss[:], 1.0 / DH, EPS, ALU.mult, ALU.add,
                       tuple("oss_h%d" % h_ for h_ in range(H)), ("orstd",))
                    act(orstd[:], orstd[:], AF.Sqrt, ("orstd",), ("orstd",))
                    SB[0].add("dve", lambda e: e.reciprocal(out=orstd[:], in_=orstd[:]), ("orstd",), ("orstd",))
                    for h in range(H):
                        stt("dve", ontmp[:, hsl(h)], o_[:, hsl(h)], orstd[:, h:h + 1], gn_bc[:], ALU.mult,
                            ALU.mult, (on_, "orstd", "gn"), ("ontmp_h%d" % h,))
                    tt("dve", g_[:], ontmp[:], z_[:], ALU.mult,
                       tuple("ontmp_h%d" % h_ for h_ in range(H)) + (zn,), (gn_,))

                def b2():
                    bT = nb()
                    pv = psb(bT)
                    tps([(pv[:, hsl(h)], g_[:, hsl(h)]) for h in range(H)], (gn_, "ident"), ("ps%d" % bT,))
                    cp("act", catT[:, 0:4, hsl(st)], pv[:, 0:512].rearrange("p (h c) -> p h c", h=H),
                       ("ps%d" % bT,), tuple("catT%d" % h for h in range(H)))
                return [b1, b2]

            RW = {"a1": 2.0, "a2": 1.0, "a3": 2.0, "b1": 2.0, "b2": 1.0}
            a_ = rec_A(0)
            back += [(a_[0], RW["a1"]), (a_[1], RW["a2"]), (a_[2], RW["a3"])]
            for st in range(1, NST):
                a_ = rec_A(st)
                b_ = rec_B(st - 1)
                back += [(a_[0], RW["a1"]), (b_[0], RW["b1"]), (a_[1], RW["a2"]), (b_[1], RW["b2"]),
                         (a_[2], RW["a3"])]
            b_ = rec_B(NST - 1)
            back += [(b_[0], RW["b1"]), (b_[1], RW["b2"])]

            def wout():
                s0, i0 = next_block(WO[0])
                s1, _ = next_block(WO[1], oldest=i0)
                Wo = [ring[s0][:].rearrange("p (r c) -> p r c", c=1024),
                      ring[s1][:].rearrange("p (r c) -> p r c", c=1024)]
                for st in range(NST):
                    for hf in range(2):
                        b = nb()
                        mms([(ps[b][:], catT[:, c, hsl(st)], Wo[c // 4][:, c % 4, hf * 512:(hf + 1) * 512],
                              c == 0, c == KC - 1) for c in range(KC)],
                            ("w%d" % s0, "w%d" % s1) + tuple("catT%d" % c for c in range(KC)), ("ps%d" % b,))
                        resid_add(b, st, hf, 1, 0)
            return back, wout

        xv = x_d.rearrange("(t s p) d -> t s p d", s=NST, p=128)
        ov = out_d.rearrange("(t s p) d -> t s p d", s=NST, p=128)
        def load_x(it, xb):
            old = cx[0]
            cx[0] = xb
            for st in range(NST):
                dma(Xb[xb][:, st, :], xv[it, st], (), xres(st), "xl%d" % st)
            cx[0] = old

        def final_norm_store(it):
            for st in range(NST):
                act(xhat[:, st, :], Xb[cx[0]][:, st, :], AF.Square, xres(st), ("xhat%d" % st, "ss%d" % st),
                    accum_out=ss[:, st:st + 1])
            ts("dve", rstd[:], ss[:], 1.0 / D, EPS, ALU.mult, ALU.add, tuple("ss%d" % st_ for st_ in range(NST)),
               ("rstd",))
            act(rstd[:], rstd[:], AF.Sqrt, ("rstd",), ("rstd",))
            SB[0].add("dve", lambda e: e.reciprocal(out=rstd[:], in_=rstd[:]), ("rstd",), ("rstd",))
            for st in range(NST):
                stt("dve", Xb[cx[0]][:, st, :], Xb[cx[0]][:, st, :], rstd[:, st:st + 1], final_bc[:],
                    ALU.mult, ALU.mult, xres(st) + ("rstd", "final_bc"), xres(st))
                dma(ov[it, st], Xb[cx[0]][:, st, :], xres(st), ("outdram",), "ost%d" % st)

        OVERLAP = os.environ.get("K_OVERLAP", "1") == "1"

        def emit_main():
            cx[0] = 0
            load_x(0, 0)
            for u in ffn_units(0, 0, False):
                u()
            for it in range(NT):
                xb = it % 2
                cx[0] = xb
                if it + 1 < NT:
                    load_x(it + 1, 1 - xb)
                barrier2()
                back, wout = mixer(it)
                barrier2()
                units = []
                if it + 1 < NT:
                    units = [with_x(1 - xb, u) for u in ffn_units(0, 0, OVERLAP)]
                if OVERLAP:
                    tot_w = max(1e-9, sum(w for _, w in back))
                    ui = 0
                    acc = 0.0
                    for fn, w in back:
                        fn()
                        acc += w * len(units) / tot_w
                        while ui < len(units) and ui < int(acc + 1e-6):
                            units[ui]()
                            ui += 1
                    while ui < len(units):
                        units[ui]()
                        ui += 1
                    wout()
                else:
                    for fn, w in back:
                        fn()
                    wout()
                    for u in units:
                        u()
                for u in ffn_units(1, 2, False):
                    u()
                final_norm_store(it)

        real_S = SB[0]
        saved_rr = rr[0]
        SB[0] = Sched()
        recording[0] = True
        emit_main()
        gseq[:] = rec_seq
        recording[0] = False
        rr[0] = saved_rr
        gi_ctr[0] = 0
        emitted[0] = 0
        mix8[0] = False
        SB[0] = real_S
        emit_main()
        SB[0].add("sp", None, ("outdram",), ())
        S = SB[0]
        S.plan()
        last_out = {}
        for op in S.ops:
            if op["chan"] is not None and op["chan"].startswith("ost"):
                last_out[op["chan"]] = op["token"]
        fin = S.ops[-1]
        for c, (key, val) in last_out.items():
            fin["waits"][key] = max(fin["waits"].get(key, 0), val)
        S.emit(nc, es)
    return nc


def _consts():
    i = np.arange(128)
    ident = np.eye(128, dtype=np.float32).astype(ml_dtypes.bfloat16)
    u32 = (i[:, None] <= i[None, :]).astype(np.float32)
    negmask = np.where(i[None, :] >= i[:, None], 0.0, NEG).astype(np.float32)
    strict = (i[None, :] > i[:, None]).astype(np.float32).astype(ml_dtypes.bfloat16)
    inv = np.zeros((128, 4, 16), np.float32)
    for g, w in enumerate((2, 4, 8, 16)):
        inv[:, g, :] = 1.0 / np.minimum(np.arange(1, 17), w)
    blk = lambda b: (i[:, None] // b) == (i[None, :] // b)
    up = i[None, :] > i[:, None]
    lo = i[None, :] < i[:, None]
    ms = [blk(16), blk(32) & ~blk(16), blk(64) & ~blk(32), ~blk(64), blk(16) & up, blk(16) & lo]
    masks = np.stack([m.astype(np.float32) for m in ms], axis=1).reshape(128, 768).astype(ml_dtypes.bfloat16)
    return ident, u32, negmask, strict, inv.reshape(128, 64), masks


def _col(v, k):
    return np.ascontiguousarray(np.asarray(v, np.float32).reshape(k, 128).T)


def make_in_map(b, x, c, w_ada, b_ada, norm_ffn1, ffn1_gate, ffn1_up, ffn1_down, norm_mix, w_in, conv_w,
                a_log, dt_bias, gdn_norm, pool_w, pool_scale, w_out, norm_ffn2, ffn2_gate, ffn2_up,
                ffn2_down, final_norm):
    ident, u32, negmask, strict, inv, masks = _consts()
    f = lambda a: np.ascontiguousarray(np.asarray(a, np.float32))
    gains = np.concatenate([_col(norm_ffn1[0], KC), _col(norm_mix[0], KC), _col(norm_ffn2[0], KC)], axis=1)
    cw = np.asarray(conv_w[0], np.float32)
    convw_col = np.ascontiguousarray(cw.T.reshape(12, 128, 4).transpose(1, 0, 2).reshape(128, 48))
    return {
        "x": f(x[b]), "c_col": _col(c[b], KC), "w_ada": f(w_ada[0]), "b_ada": f(b_ada[0]).reshape(1, -1),
        "gains_col": np.ascontiguousarray(gains), "final_norm": f(final_norm).reshape(1, -1),
        "f1g": f(ffn1_gate[0]), "f1u": f(ffn1_up[0]), "f1d": f(ffn1_down[0]),
        "f2g": f(ffn2_gate[0]), "f2u": f(ffn2_up[0]), "f2d": f(ffn2_down[0]),
        "w_in": f(w_in[0]), "convw_col": convw_col, "a_log": f(a_log[0]).reshape(1, -1),
        "dt_bias": f(dt_bias[0]).reshape(1, -1), "gdn_norm": f(gdn_norm[0]).reshape(1, -1),
        "pool_w": f(pool_w[0]), "pscale_col": _col(pool_scale[0], 4), "w_out": f(w_out[0]),
        "ident": ident, "u32": u32, "negmask": negmask, "strict01": strict, "invcnt0": inv, "masks": masks,
    }


def kernel(**inputs):
    x = np.asarray(inputs["x"])
    B, T, _ = x.shape
    DFF = np.asarray(inputs["ffn1_gate"]).shape[-1]
    nc = build_program(T=T, TT=512, DFF=DFF)
    in_maps = [make_in_map(b, **inputs) for b in range(B)]
    res = run_bass_kernel_spmd(nc, in_maps, core_ids=list(range(B)))
    return np.stack([np.asarray(r["out"], np.float32) for r in res.results], axis=0)
```
